# Optimizing a Trainium2 kernel written in Bass

```python
import jax, jax.numpy as jnp
from jax import lax
import numpy as np

D_MODEL = 1024
BATCH = 16
SEQ = 2048
DEPTH = 2
DEC_BATCH = 16
DEC_SEQ = 4096
PAST_LEN = 128

D_MIX = D_MODEL
HG_HEADS = 4
HG_DK = 128
HG_DV = (D_MIX // 2) // HG_HEADS
HG_WIDTH = HG_HEADS * HG_DV
GLA_HEADS = 4
GLA_DK = 64
GLA_DV = (D_MIX // 2) // GLA_HEADS
GLA_WIDTH = GLA_HEADS * GLA_DV
GATE_RANK = 16
GATE_NORMALIZER = 16.0
D_FF = 2816
CHUNK = 128
SUB = 16
N_SUB = CHUNK // SUB
EPS = 1e-6
LB_FLOOR = 1e-30
IN_SPLITS = (HG_HEADS * HG_DK,
             HG_HEADS * HG_DK,
             HG_HEADS * HG_DK,
             HG_WIDTH,
             HG_WIDTH,
             GLA_HEADS * GLA_DK,
             GLA_HEADS * GLA_DK,
             GLA_WIDTH,
             GATE_RANK,
             GATE_RANK,
             GLA_WIDTH)
D_IN = sum(IN_SPLITS)

kernel_name = "hymba_style_hgrn2_gla_macaron_encoder"


def rmsnorm(x, g):
    xf = x.astype(jnp.float32)
    y = xf * lax.rsqrt(jnp.mean(xf * xf, axis=-1, keepdims=True) + EPS)
    return (y * g.astype(jnp.float32)).astype(x.dtype)


def head_rmsnorm(o, g):
    B, T, H, d = o.shape
    y = o * lax.rsqrt(jnp.mean(o * o, axis=-1, keepdims=True) + EPS)
    return (y * g.astype(jnp.float32).reshape(H, d)).reshape(B, T, H * d)


def swiglu(x, w_in, w_out):
    a, b = jnp.split(x @ w_in, 2, axis=-1)
    return (jax.nn.silu(a) * b) @ w_out


def masked_exp(mask, e):
    return jnp.where(mask, jnp.exp(jnp.where(mask, e, 0.0)), 0.0)


def gated_linear_scan(q, k, v, g):
    f32 = jnp.float32
    q, k, v, g = (a.astype(f32) for a in (q, k, v, g))
    B, T, H, DK = q.shape
    DV = v.shape[-1]
    N = T // CHUNK

    def to_chunks(a):
        return jnp.moveaxis(a.reshape(B, N, CHUNK, H, a.shape[-1]), 1, 0)

    pos = jnp.arange(CHUNK)
    off_mask = pos[None, :] < (jnp.arange(N_SUB) * SUB)[:, None]
    diag_mask = jnp.tril(jnp.ones((SUB, SUB), dtype=bool))

    def step(S, inp):
        qi, ki, vi, gi = inp
        b = jnp.cumsum(gi, axis=1)
        b_last = b[:, -1]
        o_state = jnp.einsum('bchk,bhkv->bchv', qi * jnp.exp(b), S)
        S_new = jnp.exp(b_last)[..., None] * S + jnp.einsum(
            'bchk,bchv->bhkv', ki * jnp.exp(b_last[:, None] - b), vi)
        qs = qi.reshape(B, N_SUB, SUB, H, DK)
        ks = ki.reshape(B, N_SUB, SUB, H, DK)
        vs = vi.reshape(B, N_SUB, SUB, H, DV)
        bs = b.reshape(B, N_SUB, SUB, H, DK)
        ref = jnp.concatenate([jnp.zeros((B, 1, H, DK), f32), b[:, SUB - 1:CHUNK - 1:SUB]], axis=1)
        q_off = qs * jnp.exp(bs - ref[:, :, None])
        k_off = ki[:, None] * masked_exp(off_mask[None, :, :, None, None],
                                         ref[:, :, None] - b[:, None])
        a_off = jnp.einsum('bnlhk,bnchk->bnhlc', q_off, k_off)
        o_off = jnp.einsum('bnhlc,bchv->bnlhv', a_off, vi)
        d_dec = masked_exp(diag_mask[None, None, :, :, None, None],
                           bs[:, :, :, None] - bs[:, :, None, :])
        a_diag = jnp.einsum('bnthk,bnshk,bntshk->bnhts', qs, ks, d_dec)
        o_diag = jnp.einsum('bnhts,bnshv->bnthv', a_diag, vs)
        o = o_state + (o_off + o_diag).reshape(B, CHUNK, H, DV)
        return S_new, o

    S0 = jnp.zeros((B, H, DK, DV), f32)
    _, outs = lax.scan(step, S0, tuple(to_chunks(a) for a in (q, k, v, g)))
    return jnp.moveaxis(outs, 0, 1).reshape(B, T, H, DV)


def bidirectional_scan(q, k_fwd, k_bwd, v, g_fwd, g_bwd):
    flip = lambda a: jnp.flip(a, axis=1)
    fwd = gated_linear_scan(q, k_fwd, v, g_fwd)
    bwd = flip(gated_linear_scan(flip(q), flip(k_bwd), flip(v), flip(g_bwd)))
    return fwd + bwd


def hgrn_log_forget(z, lb):
    return jnp.logaddexp(jnp.log(jnp.maximum(lb, LB_FLOOR)), jnp.log1p(-lb) + jax.nn.log_sigmoid(z))


def encoder_layer(x, lb, n1, w1i, w1o, nm, wi, wg, bg, hgn, glan, wo, n2, w2i, w2o):
    B, T, _ = x.shape
    x = x + 0.5 * swiglu(rmsnorm(x, n1), w1i, w1o)
    h = rmsnorm(x, nm)
    p = (h @ wi).astype(jnp.float32)
    bounds = np.cumsum(IN_SPLITS)[:-1].tolist()
    hq, hf_f, hf_b, hi, hg, gq, gk, gv, ga_f, ga_b, gr = jnp.split(p, bounds, axis=-1)
    heads = lambda a, n: a.reshape(B, T, n, -1)
    hd = HG_HEADS * HG_DK
    logf_f = hgrn_log_forget(hf_f, lb[:hd])
    logf_b = hgrn_log_forget(hf_b, lb[hd:])
    o_hg = bidirectional_scan(heads(hq, HG_HEADS),
                              heads(-jnp.expm1(logf_f), HG_HEADS), heads(-jnp.expm1(logf_b), HG_HEADS),
                              heads(hi, HG_HEADS),
                              heads(logf_f, HG_HEADS), heads(logf_b, HG_HEADS))
    o_hg = head_rmsnorm(o_hg, hgn) * jax.nn.sigmoid(hg)
    la_f = jax.nn.log_sigmoid(ga_f @ wg[0].astype(jnp.float32) + bg[0].astype(jnp.float32)) / GATE_NORMALIZER
    la_b = jax.nn.log_sigmoid(ga_b @ wg[1].astype(jnp.float32) + bg[1].astype(jnp.float32)) / GATE_NORMALIZER
    k_gla = heads(gk, GLA_HEADS)
    o_gla = bidirectional_scan(heads(gq * GLA_DK ** -0.5, GLA_HEADS), k_gla, k_gla,
                               heads(gv, GLA_HEADS),
                               heads(la_f, GLA_HEADS), heads(la_b, GLA_HEADS))
    o_gla = head_rmsnorm(o_gla, glan) * jax.nn.silu(gr)
    x = x + jnp.concatenate([o_hg, o_gla], axis=-1).astype(x.dtype) @ wo
    x = x + 0.5 * swiglu(rmsnorm(x, n2), w2i, w2o)
    return x


def setup_inputs(seed: int = 0) -> dict:
    key = jax.random.key(seed)
    ks = jax.random.split(key, 18)
    nrm = lambda k, shape, scale: jax.random.normal(k, shape, jnp.float32) * scale
    gain = lambda k, shape: 1.0 + 0.02 * jax.random.normal(k, shape, jnp.float32)
    return {
        "x_prompt": nrm(ks[0], (BATCH, SEQ, D_MODEL), 1.0),
        "x_sample": nrm(ks[1], (DEC_BATCH, DEC_SEQ, D_MODEL), 1.0),
        "lower_bounds": nrm(ks[2], (DEPTH, 2 * HG_HEADS * HG_DK), 0.1),
        "ffn1_norm": gain(ks[3], (DEPTH, D_MODEL)),
        "ffn1_w_in": nrm(ks[4], (DEPTH, D_MODEL, 2 * D_FF), D_MODEL ** -0.5),
        "ffn1_w_out": nrm(ks[5], (DEPTH, D_FF, D_MODEL), D_FF ** -0.5),
        "mix_norm": gain(ks[6], (DEPTH, D_MODEL)),
        "w_in": nrm(ks[7], (DEPTH, D_MODEL, D_IN), D_MODEL ** -0.5),
        "gla_w_gate": nrm(ks[8], (DEPTH, 2, GATE_RANK, GLA_HEADS * GLA_DK), GATE_RANK ** -0.5),
        "gla_b_gate": nrm(ks[9], (DEPTH, 2, GLA_HEADS * GLA_DK), 0.01),
        "hg_head_norm": gain(ks[10], (DEPTH, HG_WIDTH)),
        "gla_head_norm": gain(ks[11], (DEPTH, GLA_WIDTH)),
        "w_out": nrm(ks[12], (DEPTH, D_MIX, D_MODEL), D_MIX ** -0.5),
        "ffn2_norm": gain(ks[13], (DEPTH, D_MODEL)),
        "ffn2_w_in": nrm(ks[14], (DEPTH, D_MODEL, 2 * D_FF), D_MODEL ** -0.5),
        "ffn2_w_out": nrm(ks[15], (DEPTH, D_FF, D_MODEL), D_FF ** -0.5),
        "final_norm": gain(ks[16], (D_MODEL,)),
    }


def reference(x_prompt, x_sample, lower_bounds, ffn1_norm, ffn1_w_in, ffn1_w_out, mix_norm, w_in,
              gla_w_gate, gla_b_gate, hg_head_norm, gla_head_norm, w_out, ffn2_norm, ffn2_w_in,
              ffn2_w_out, final_norm):
    probs = jax.nn.softmax(lower_bounds.astype(jnp.float32), axis=0)
    lbs = jnp.cumsum(probs, axis=0) - probs[0]

    def trunk(x):
        for l in range(DEPTH):
            x = encoder_layer(x, lbs[l], ffn1_norm[l], ffn1_w_in[l], ffn1_w_out[l], mix_norm[l], w_in[l],
                              gla_w_gate[l], gla_b_gate[l], hg_head_norm[l], gla_head_norm[l], w_out[l],
                              ffn2_norm[l], ffn2_w_in[l], ffn2_w_out[l])
        return rmsnorm(x, final_norm)

    y_prompt = trunk(x_prompt)
    y_sample = trunk(x_sample)
    return (y_prompt, y_sample)
```

```python
import numpy as np
from contextlib import ExitStack
import concourse.bass as bass
import concourse.mybir as mybir
from concourse.bass_utils import run_bass_kernel_spmd

F32 = mybir.dt.float32
BF16 = mybir.dt.bfloat16
AF = mybir.ActivationFunctionType
ALU = mybir.AluOpType

D = 1024
DFF = 2816
DIN = 4128
L = 2
TT = 512
NB = TT // 128
CH = 64
NCH = TT // CH
EPS = 1e-6
NSLOT = 4
WI_GROUPS = [(0, 512), (512, 512), (1024, 512), (1536, 512), (2048, 512), (2560, 512), (3072, 512),
             (3584, 144), (3616, 512)]
G_Q, G_ZF, G_ZB, G_VH, G_HG, G_GQK, G_VG, G_GA, G_GR = range(9)


class Buf:
    __slots__ = ("name", "w", "r", "sem", "cnt")

    def __init__(self, name):
        self.name = name
        self.w = None
        self.r = []
        self.sem = None
        self.cnt = 0


class Op:
    __slots__ = ("eng", "fn", "deps", "dma", "sem", "val", "sig", "users", "ndma")

    def __init__(self, eng, fn):
        self.eng = eng
        self.fn = fn
        self.deps = []
        self.dma = False
        self.sem = None
        self.val = 0
        self.sig = 0
        self.users = 0
        self.ndma = 0


ENGS = ("sp", "act", "pool", "pe", "dve")


class Em:
    def __init__(self, nc, es):
        self.nc = nc
        self.es = es
        self.ops = {e: [] for e in ENGS}
        self.esem = {e: es.enter_context(nc.semaphore("s_" + e)) for e in ENGS if e != "sp"}
        self.nsem = 0
        self.bar = {e: [] for e in ENGS}
        self.all_dma = []

    def _dep(self, op, p):
        if p is None or p is op:
            return
        if p.eng == op.eng and not p.dma:
            if op.eng == "pe" or op.eng == "sp":
                return
        op.deps.append(p)

    def op(self, eng, fn, reads=(), writes=(), same_raw_only=False):
        o = Op(eng, fn)
        for b in reads:
            self._dep(o, b.w)
        for b in writes:
            if not (b.w is not None and b.w.eng == eng and not b.w.dma and same_raw_only):
                self._dep(o, b.w)
            for r in b.r:
                if r.eng == eng and not r.dma and same_raw_only:
                    continue
                self._dep(o, r)
        for p in self.bar[eng]:
            self._dep(o, p)
        self.bar[eng] = []
        for b in reads:
            b.r.append(o)
        for b in writes:
            b.w = o
            b.r = []
        self.ops[eng].append(o)
        return o

    def dma(self, fn, ndma, reads=(), writes=(), owner=None):
        o = self.op("sp", fn, reads, writes)
        o.dma = True
        o.ndma = ndma
        if owner.sem is None:
            owner.sem = self.es.enter_context(self.nc.semaphore("d%d" % self.nsem))
            self.nsem += 1
        owner.cnt += 16 * ndma
        o.sem = owner.sem
        o.val = owner.cnt
        self.all_dma.append(o)
        return o

    @staticmethod
    def inherit(child, parent):
        child.w = parent.w
        child.r = list(parent.r)

    @staticmethod
    def merge(parent, children):
        rs = list(parent.r)
        for c in children:
            rs += list(c.r)
            if c.w is not None:
                rs.append(c.w)
        parent.r = rs

    def barrier(self):
        lasts = [self.ops[e][-1] for e in ENGS if self.ops[e] and e != "sp"]
        lat = {}
        for o in self.all_dma:
            lat[id(o.sem)] = o
        for e in ENGS:
            self.bar[e] = lasts + list(lat.values())

    def emit(self):
        nc = self.nc
        for e in ENGS:
            for o in self.ops[e]:
                for p in o.deps:
                    p.users += 1
        for e in ENGS:
            if e == "sp":
                continue
            k = 0
            for o in self.ops[e]:
                if o.users > 0:
                    k += 1
                    o.sig = k
                    o.sem = self.esem[e]
                    o.val = k
        engobj = {"sp": nc.sync, "act": nc.scalar, "pool": nc.gpsimd, "pe": nc.tensor, "dve": nc.vector}
        finals = {}
        for o in self.all_dma:
            finals[id(o.sem)] = o

        def run(e, eng):
            known = {}
            for o in self.ops[e]:
                need = {}
                for p in o.deps:
                    key = id(p.sem)
                    if known.get(key, 0) >= p.val:
                        continue
                    if key not in need or need[key][1] < p.val:
                        need[key] = (p.sem, p.val)
                for key, (sem, val) in need.items():
                    eng.wait_ge(sem, val)
                    known[key] = val
                if o.dma:
                    o.fn(eng, o.sem)
                else:
                    ins = o.fn(eng)
                    if o.users > 0:
                        ins.then_inc(o.sem, 1)
            if e == "sp":
                for o in finals.values():
                    if known.get(id(o.sem), 0) < o.val:
                        eng.wait_ge(o.sem, o.val)

        with nc.Block() as block:
            @block.sync
            def _(eng):
                run("sp", eng)

            @block.scalar
            def _(eng):
                run("act", eng)

            @block.gpsimd
            def _(eng):
                run("pool", eng)

            @block.tensor
            def _(eng):
                run("pe", eng)

            @block.vector
            def _(eng):
                run("dve", eng)


class _Stop(Exception):
    pass


DEBUG_STOP = [None]


def _chk(tag):
    if DEBUG_STOP[0] == tag:
        raise _Stop()


def build_program(seq_lens_p, seq_lens_s, n_layers=L):
    nc = bass.Bass("TRN2", target_bir_lowering=False)
    es = ExitStack()
    em = Em(nc, es)
    try:
        _build(nc, es, em, seq_lens_p, seq_lens_s, n_layers)
    except _Stop:
        pass
    em.emit()
    es.close()
    return nc


def _build(nc, es, em, seq_lens_p, seq_lens_s, n_layers=L):
    NP, TP = seq_lens_p
    NS, TS = seq_lens_s

    def din(name, shape, dt=F32):
        return nc.dram_tensor(name, list(shape), dt, kind="ExternalInput").ap()

    def dout(name, shape, dt=F32):
        return nc.dram_tensor(name, list(shape), dt, kind="ExternalOutput").ap()

    def dscr(name, shape, dt):
        return nc.dram_tensor(name, list(shape), dt, kind="Internal").ap()

    xp = din("xp", [NP, TP, D])
    xsm = din("xs", [NS, TS, D])
    yp = dout("yp", [NP, TP, D])
    ysm = dout("ys", [NS, TS, D])
    lower_bounds = din("lower_bounds", [L, 1024])
    ffn1_norm = din("ffn1_norm", [L, D])
    ffn1_w_in = din("ffn1_w_in", [L, D, 2 * DFF])
    ffn1_w_out = din("ffn1_w_out", [L, DFF, D])
    mix_norm = din("mix_norm", [L, D])
    w_in = din("w_in", [L, D, DIN])
    gla_w_gate = din("gla_w_gate", [L, 2, 16, 256])
    gla_b_gate = din("gla_b_gate", [L, 2, 256])
    hg_head_norm = din("hg_head_norm", [L, 512])
    gla_head_norm = din("gla_head_norm", [L, 512])
    w_out = din("w_out", [L, D, D])
    ffn2_norm = din("ffn2_norm", [L, D])
    ffn2_w_in = din("ffn2_w_in", [L, D, 2 * DFF])
    ffn2_w_out = din("ffn2_w_out", [L, DFF, D])
    final_norm = din("final_norm", [D])
    c_ident = din("c_ident", [128, 128])
    c_mask = din("c_mask", [128, 2, CH])
    c_rmask = din("c_rmask", [128, 2 * TT], BF16)

    seqs = []
    for i in range(NP):
        seqs.append((TP, xp[i], yp[i]))
    for i in range(NS):
        seqs.append((TS, xsm[i], ysm[i]))

    NG1 = (2 * DFF) // 512
    s_win = {}
    for nm in ("f1", "f2"):
        s_win[nm] = [dscr("sw_%s_%d" % (nm, l), [NG1, 128, 8 * 512], BF16) for l in range(n_layers)]
    s_wi = [[dscr("sw_wi_%d_%d" % (l, g), [128, 8 * WI_GROUPS[g][1]], BF16) for g in range(9)] for l in range(n_layers)]
    s_wout = {}
    for nm in ("f1", "f2"):
        s_wout[nm] = [dscr("so_%s_%d" % (nm, l), [DFF, D], BF16) for l in range(n_layers)]
    s_wo = [dscr("so_wo_%d" % l, [D, D], BF16) for l in range(n_layers)]
    s_x = [dscr("sx_%d" % i, [T, D], F32) for i, (T, _, _) in enumerate(seqs)]
    s_of = [dscr("sof_%d" % i, [T // 128, 128, 1024], F32) for i, (T, _, _) in enumerate(seqs)]
    QKV_N = 4 * TT + 2 * TT + 2 * TT + NB * 1024
    s_qkv = [dscr("sqkv_%d" % i, [T // TT, 128, QKV_N], BF16) for i, (T, _, _) in enumerate(seqs)]
    d_qkv = [[Buf("dqkv%d_%d" % (i, k)) for k in range(4)] for i in range(len(seqs))]
    d_x = [Buf("dx%d" % i) for i in range(len(seqs))]
    d_of = [Buf("dof%d" % i) for i in range(len(seqs))]
    d_w = None

    def sb(name, shape, dt):
        return es.enter_context(nc.sbuf_tensor(name, list(shape), dt))

    xbs = [sb("xb%d" % i, [128, NB, D], F32) for i in range(2)]
    B_xbs = [Buf("xb%d" % i) for i in range(2)]
    hid = sb("hid", [128, 22 * TT], BF16); B_hid = Buf("hid")
    hT = sb("hT", [128, 8, TT], BF16); B_hT = Buf("hT")
    wring = [sb("wr%d" % i, [128, 4096], BF16) for i in range(NSLOT)]
    B_wr = [Buf("wr%d" % i) for i in range(NSLOT)]
    cv32 = [xbs[i][:, :, :].rearrange("p b d -> p (b d)") for i in range(2)]
    B_cv32 = B_xbs
    NSET = 2
    t_sg_l = [sb("t_sg%d" % i, [128, 2, TT], F32) for i in range(NSET)]; B_sg_l = [Buf("t_sg%d" % i) for i in range(NSET)]
    t_g_l = [sb("t_g%d" % i, [128, 2, TT], F32) for i in range(NSET)]; B_g_l = [Buf("t_g%d" % i) for i in range(NSET)]
    t_b_l = [sb("t_b%d" % i, [128, 2, TT], F32) for i in range(NSET)]; B_b_l = [Buf("t_b%d" % i) for i in range(NSET)]
    t_k_l = [sb("t_k%d" % i, [128, 2, TT], BF16) for i in range(NSET)]; B_k_l = [Buf("t_k%d" % i) for i in range(NSET)]
    qh = sb("qh", [128, 4, TT], BF16); B_qh = Buf("qh")
    qg = sb("qg", [128, 2, TT], BF16); B_qg = Buf("qg")
    kg = sb("kg", [128, 2, TT], BF16); B_kg = Buf("kg")
    gaT = sb("gaT", [128, TT], BF16); B_gaT = Buf("gaT")
    QdT = [sb("QdT%d" % g, [128, 2, TT], BF16) for g in range(3)]; B_Qd = [Buf("Qd%d" % g) for g in range(3)]
    KdT = [sb("KdT%d" % g, [128, 2, TT], BF16) for g in range(3)]; B_Kd = [Buf("Kd%d" % g) for g in range(3)]
    Qz = [sb("Qz%d" % h, [128, TT], BF16) for h in range(4)]; B_Qz = [Buf("Qz%d" % h) for h in range(4)]
    t_A_l = [sb("t_A%d" % i, [128, 2, NCH], F32) for i in range(NSET)]; B_A_l = [Buf("t_A%d" % i) for i in range(NSET)]
    t_tot_l = [sb("t_tot%d" % i, [128, 2, NCH], F32) for i in range(NSET)]; B_tot_l = [Buf("t_tot%d" % i) for i in range(NSET)]
    t_tmA_l = [sb("t_tmA%d" % i, [128, 2, NCH], F32) for i in range(NSET)]; B_tmA_l = [Buf("t_tmA%d" % i) for i in range(NSET)]
    esc = [sb("esc%d" % g, [128, 3, 2, NCH], F32) for g in range(3)]; B_esc = [Buf("esc%d" % g) for g in range(3)]
    Vt = sb("Vt", [128, NB, 1024], BF16); B_Vt = Buf("Vt")
    Kc = [sb("Kc%d" % c, [128, 768], BF16) for c in range(2)]
    B_Kc = [[Buf("Kc%d_%d" % (c, h)) for h in range(2)] for c in range(2)]
    ATs = sb("ATs", [128, 8, 128], BF16); B_ATs = [Buf("ATs0"), Buf("ATs1")]
    S = sb("S", [128, 6, 128], F32); B_S = [Buf("S%d" % i) for i in range(6)]
    Sp = sb("Sp", [128, 6, 128], BF16); B_Sp = [Buf("Sp%d" % i) for i in range(6)]
    osb = sb("osb", [128, 8, 128], F32); B_osb = Buf("osb")
    ofl = sb("ofl", [128, 8, 128], F32); B_ofl = Buf("ofl")
    sqb = sb("sqb", [128, 8, 128], BF16); B_sqb = Buf("sqb")
    rsb = sb("rsb", [128, 8, 128], F32); B_rsb = Buf("rsb")
    gateT = sb("gateT", [128, 8, TT], BF16); B_gate = Buf("gateT")
    onT = sb("onT", [128, 8, TT], BF16); B_onT = Buf("onT")
    nstat = sb("nstat", [128, 8], F32); B_nstat = Buf("nstat")
    nrstd = sb("nrstd", [128, 8], F32); B_nrstd = Buf("nrstd")
    B_c = Buf("consts")
    ident32 = sb("ident32", [128, 128], F32)
    ident = sb("ident", [128, 128], BF16)
    ones_b = sb("ones_b", [128, 128], BF16)
    mask = sb("mask", [128, 2, CH], F32)
    rmask = sb("rmask", [128, 2 * TT], BF16)
    epsc = sb("epsc", [128, 1], F32)
    g_n1 = sb("g_n1", [128, L, 8], F32)
    g_nm = sb("g_nm", [128, L, 8], F32)
    g_n2 = sb("g_n2", [128, L, 8], F32)
    g_hn = sb("g_hn", [128, L, 8], F32)
    g_fin = sb("g_fin", [128, D], F32)
    lbraw = sb("lbraw", [128, L, 8], F32)
    lbT = sb("lbT", [128, L, 8], F32)
    omlT = sb("omlT", [128, L, 8], F32)
    nomlT = sb("nomlT", [128, L, 8], F32)
    bgT = sb("bgT", [128, L, 2, 2], F32)
    wg32 = xbs[1][0:16, 0, :].rearrange("p (l d c) -> p l d c", l=L, d=2)
    wgb = sb("wgb", [128, L, 2, 256], BF16)
    psum = es.enter_context(nc.psum_tensor("psum", [128, 8 * 512], F32))
    B_ps = [Buf("ps%d" % i) for i in range(8)]

    def bank(i):
        return psum[:, i * 512:(i + 1) * 512]

    def bank16(i):
        return psum[:, i * 512:(i + 1) * 512].bitcast(BF16)

    def load_consts(sp, sem):
        def ld(out, in_):
            sp.dma_start(out=out, in_=in_).then_inc(sem, 16)
        with nc.allow_non_contiguous_dma(reason="tiny param loads"):
            ld(ident32[:], c_ident[:, :])
            ld(mask[:], c_mask[:, :, :])
            ld(rmask[:], c_rmask[:, :])
            ld(g_n1[:], ffn1_norm.rearrange("l (j p) -> p l j", p=128))
            ld(g_nm[:], mix_norm.rearrange("l (j p) -> p l j", p=128))
            ld(g_n2[:], ffn2_norm.rearrange("l (j p) -> p l j", p=128))
            for l_ in range(L):
                ld(g_hn[:, l_, 0:4], hg_head_norm[l_].rearrange("(j p) -> p j", p=128))
                ld(g_hn[:, l_, 4:8], gla_head_norm[l_].rearrange("(j p) -> p j", p=128))
            ld(g_fin[:], final_norm.partition_broadcast(128))
            ld(lbraw[:], lower_bounds.rearrange("l (j p) -> p l j", p=128))
            for l_ in range(L):
                for d_ in range(2):
                    ld(bgT[:, l_, d_, :], gla_b_gate[l_, d_].rearrange("(j p) -> p j", p=128))
            ld(wg32, gla_w_gate.rearrange("l d r c -> r l d c"))
    em.dma(load_consts, 9 + 2 * L + 2 * L, writes=[B_c, B_xbs[1]], owner=B_c)
    B_c2 = Buf("consts2")
    em.op("dve", lambda e: e.tensor_copy(out=ident[:], in_=ident32[:]), reads=[B_c], writes=[B_c2])
    em.op("dve", lambda e: e.memset(ones_b[:], 1.0), writes=[B_c2])
    em.op("dve", lambda e: e.memset(epsc[:], EPS), writes=[B_c2])
    em.op("dve", lambda e: e.memset(wgb[:], 0.0), writes=[B_c2])
    B_c3 = Buf("consts3")
    em.op("dve", lambda e: e.tensor_copy(out=wgb[0:16], in_=wg32), reads=[B_c, B_c2, B_xbs[1]], writes=[B_c3])
    B_lb = Buf("lb")
    tmpl = sb("tmpl", [128, 8], F32)
    em.op("dve", lambda e: e.tensor_tensor(out=tmpl[:], in0=lbraw[:, 0, :], in1=lbraw[:, 1, :], op=ALU.subtract),
          reads=[B_c], writes=[B_lb])
    em.op("act", lambda e: e.activation(out=tmpl[:], in_=tmpl[:], func=AF.Exp), reads=[B_lb], writes=[B_lb])
    em.op("dve", lambda e: e.tensor_scalar(out=tmpl[:], in0=tmpl[:], scalar1=1.0, scalar2=None, op0=ALU.add),
          reads=[B_lb], writes=[B_lb])
    em.op("dve", lambda e: e.memset(lbT[:], 0.0), writes=[B_lb])
    em.op("dve", lambda e: e.reciprocal(out=lbT[:, 1, :], in_=tmpl[:]), reads=[B_lb], writes=[B_lb])
    em.op("dve", lambda e: e.tensor_scalar(out=omlT[:], in0=lbT[:], scalar1=-1.0, scalar2=1.0, op0=ALU.mult, op1=ALU.add),
          reads=[B_lb], writes=[B_lb])
    em.op("dve", lambda e: e.tensor_scalar(out=nomlT[:], in0=omlT[:], scalar1=-1.0, scalar2=None, op0=ALU.mult),
          reads=[B_lb], writes=[B_lb])
    for h in range(4):
        em.op("pool", lambda e, h=h: e.memset(Qz[h][:], 0.0), writes=[B_Qz[h]])
    for c in range(2):
        em.op("pool", lambda e, c=c: e.memset(Kc[c][:], 0.0), writes=B_Kc[c])
    em.op("pool", lambda e: e.memset(ATs[:], 0.0), writes=B_ATs)
    em.op("pool", lambda e: e.memset(S[:], 0.0), writes=B_S)
    em.op("pool", lambda e: e.memset(Sp[:], 0.0), writes=B_Sp)
    em.op("pool", lambda e: e.memset(gaT[:], 0.0), writes=[B_gaT])

    _chk("consts")
    cvt_i = [0]
    d_wh = [None]
    cast_engs = ("dve", "act", "pool")

    def convert(src_ap, dst_ap, n):
        i = cvt_i[0]
        cvt_i[0] += 1
        s32, b32 = cv32[i % 2], B_cv32[i % 2]
        k = i % 2
        shp = list(src_ap.shape[1:])
        if len(shp) == 2:
            v32 = s32[:, 0:n].rearrange("p (a b) -> p a b", a=shp[0])
            v16 = hid[:, k * 4096:k * 4096 + n].rearrange("p (a b) -> p a b", a=shp[0])
        else:
            v32 = s32[:, 0:n]
            v16 = hid[:, k * 4096:k * 4096 + n]
        b16 = B_cvh[k]
        em.dma(lambda sp, sem: sp.dma_start(out=v32, in_=src_ap).then_inc(sem, 16), 1, writes=[b32], owner=b32)
        ce = cast_engs[i % 3]
        if ce == "act":
            em.op("act", lambda e: e.activation(out=v16, in_=v32, func=AF.Copy), reads=[b32], writes=[b16])
        else:
            em.op(ce, lambda e: e.tensor_copy(out=v16, in_=v32), reads=[b32], writes=[b16])
        em.dma(lambda sp, sem: sp.dma_start(out=dst_ap, in_=v16).then_inc(sem, 16), 1, reads=[b16], writes=[d_wh[0]], owner=b16)

    B_cvh = [Buf("cvh0"), Buf("cvh1")]
    d_ws = {(l, k): Buf("dw_%d_%s" % (l, k)) for l in range(n_layers) for k in ("f1i", "f1o", "f2i", "f2o", "wi", "wo")}
    jobs = {}
    for l in range(n_layers):
        for nm, wsrc in (("f1", ffn1_w_in), ("f2", ffn2_w_in)):
            wv = wsrc[l].rearrange("(kc p) n -> p kc n", p=128)
            jobs[(l, nm + "i")] = [(wv[:, :, g * 512:(g + 1) * 512], s_win[nm][l][g].rearrange("p (kc n) -> p kc n", kc=8), 4096)
                                   for g in range(NG1)]
        wv = w_in[l].rearrange("(kc p) n -> p kc n", p=128)
        jobs[(l, "wi")] = [(wv[:, :, c0:c0 + ncol], s_wi[l][g].rearrange("p (kc n) -> p kc n", kc=8), 8 * ncol)
                           for g, (c0, ncol) in enumerate(WI_GROUPS)]
        for nm, wsrc in (("f1", ffn1_w_out), ("f2", ffn2_w_out)):
            wv = wsrc[l].rearrange("(kc p) n -> p kc n", p=128)
            dv = s_wout[nm][l].rearrange("(kc p) n -> p kc n", p=128)
            jobs[(l, nm + "o")] = [(wv[:, k0:k0 + min(4, 22 - k0), :], dv[:, k0:k0 + min(4, 22 - k0), :], min(4, 22 - k0) * 1024)
                                   for k0 in range(0, 22, 4)]
        wv = w_out[l].rearrange("(kc p) n -> p kc n", p=128)
        dv = s_wo[l].rearrange("(kc p) n -> p kc n", p=128)
        jobs[(l, "wo")] = [(wv[:, k0:k0 + 4, :], dv[:, k0:k0 + 4, :], 4096) for k0 in range(0, 8, 4)]
    eager_keys = [(0, "f1i"), (0, "f1o"), (0, "wi")]
    lazy_keys = [(0, "wo"), (0, "f2i"), (0, "f2o")] + [(l, k) for l in range(1, n_layers) for k in ("f1i", "f1o", "wi", "wo", "f2i", "f2o")]
    for key in eager_keys:
        d_wh[0] = d_ws[key]
        for (sa, da, n) in jobs[key]:
            convert(sa, da, n)

    def lazy_convert():
        xflat = xbs[1][:, :, :].rearrange("p b d -> p (b d)")
        gflat = gateT[:, :, :].rearrange("p u t -> p (u t)")
        i = 0
        for key in lazy_keys:
            dbuf = d_ws[key]
            for (sa, da, n) in jobs[key]:
                a3 = list(sa.shape[1:])[0]
                v32 = xflat[:, 0:n].rearrange("p (a b) -> p a b", a=a3)
                v16 = gflat[:, 0:n].rearrange("p (a b) -> p a b", a=a3)
                em.dma(lambda sp, sem, v32=v32, sa=sa: sp.dma_start(out=v32, in_=sa).then_inc(sem, 16), 1,
                       writes=[B_xbs[1]], owner=B_xbs[1])
                em.op("pool" if i % 2 else "dve", lambda e, v16=v16, v32=v32: e.tensor_copy(out=v16, in_=v32),
                      reads=[B_xbs[1]], writes=[B_gate])
                em.dma(lambda sp, sem, v16=v16, da=da: sp.dma_start(out=da, in_=v16).then_inc(sem, 16), 1,
                       reads=[B_gate], writes=[dbuf], owner=B_gate)
                i += 1
                yield
    em.barrier()
    _chk("convert")

    wr_i = [0]

    def wload(src_ap, n, shape3, dbuf):
        i = wr_i[0] % NSLOT
        wr_i[0] += 1
        view = wring[i][:, 0:n].rearrange("p (a b) -> p a b", a=shape3[0])
        em.dma(lambda sp, sem: sp.dma_start(out=view, in_=src_ap).then_inc(sem, 16), 1,
               reads=[dbuf], writes=[B_wr[i]], owner=B_wr[i])
        return B_wr[i], view

    mm_rot = [0]

    MMB = (4, 5, 6, 7)

    def nextbank():
        pb = MMB[mm_rot[0] % 4]
        mm_rot[0] += 1
        return pb

    def norm_to_hT(gain, l, xb, B_xb):
        xn = hid[:, 0:NB * 1024].rearrange("p (b d) -> p b d", b=NB)
        for b in range(NB):
            junk = hid[:, (NB + b) * 1024:(NB + b) * 1024 + 1024]
            em.op("act", lambda e, b=b, junk=junk: e.activation(out=junk, in_=xb[:, b, :], func=AF.Square,
                                                     accum_out=nstat[:, b:b + 1]),
                  reads=[B_xb], writes=[B_hid, B_nstat])
        em.op("act", lambda e: e.activation(out=nrstd[:, 0:NB], in_=nstat[:, 0:NB], func=AF.Ln, scale=1.0 / D, bias=epsc[:]),
              reads=[B_nstat, B_c2], writes=[B_nrstd])
        em.op("act", lambda e: e.activation(out=nrstd[:, 0:NB], in_=nrstd[:, 0:NB], func=AF.Exp, scale=-0.5),
              reads=[B_nrstd], writes=[B_nrstd])
        for b in range(NB):
            em.op("dve", lambda e, b=b: e.tensor_scalar(out=xn[:, b, :], in0=xb[:, b, :], scalar1=nrstd[:, b:b + 1],
                                                        scalar2=None, op0=ALU.mult),
                  reads=[B_xb, B_nrstd], writes=[B_hid])
        yield
        for kc in range(8):
            pb = 4 + (kc % 2)
            half = bank16(pb)[:, 0:TT]
            for b in range(NB):
                em.op("pe", lambda e, b=b, kc=kc, half=half: e.transpose(out=half[:, b * 128:(b + 1) * 128],
                                                                         in_=xn[:, b, kc * 128:(kc + 1) * 128],
                                                                         identity=ident[:]),
                      reads=[B_hid, B_c2], writes=[B_ps[pb]])
            if kc % 2 == 0:
                em.op("act", lambda e, kc=kc, half=half: e.activation(out=hT[:, kc, :], in_=half, func=AF.Copy,
                                                                      scale=gain[:, l, kc:kc + 1]),
                      reads=[B_ps[pb], B_c], writes=[B_hT])
            else:
                em.op("dve", lambda e, kc=kc, half=half: e.tensor_scalar(out=hT[:, kc, :], in0=half,
                                                                         scalar1=gain[:, l, kc:kc + 1], scalar2=None, op0=ALU.mult),
                      reads=[B_ps[pb], B_c], writes=[B_hT])
        yield

    def fmajor_mm(slot_view, c, pb, ncols=128, col0=None):
        c0 = c * 128 if col0 is None else col0
        for kc in range(8):
            em.op("pe", lambda e, kc=kc: e.matmul(bank(pb)[0:ncols, :], lhsT=slot_view[:, kc, c0:c0 + ncols],
                                                  rhs=hT[:, kc, :], start=(kc == 0), stop=(kc == 7)),
                  reads=[cur_w[0], B_hT], writes=[B_ps[pb]])

    cur_w = [None]

    def ffn(nm, l, xb, B_xb):
        hv = hid[:, :].rearrange("p (j t) -> p j t", j=22)
        for g in range(NG1):
            bw, wv = wload(s_win[nm][l][g].rearrange("p (kc n) -> p kc n", kc=8), 4096, (8, 512), d_ws[(l, nm + "i")])
            cur_w[0] = bw
            for c in range(4):
                j = g * 4 + c
                pb = nextbank()
                fmajor_mm(wv, c, pb)
                if j < 22:
                    em.op("act", lambda e, j=j, pb=pb: e.activation(out=hv[:, j, :], in_=bank(pb), func=AF.Silu),
                          reads=[B_ps[pb]], writes=[B_hid])
                else:
                    em.op("dve", lambda e, j=j, pb=pb: e.tensor_tensor(out=hv[:, j - 22, :], in0=bank(pb),
                                                                        in1=hv[:, j - 22, :], op=ALU.mult),
                          reads=[B_ps[pb], B_hid], writes=[B_hid])
            yield
        wov = s_wout[nm][l].rearrange("(kc p) n -> p kc n", p=128)
        for hf in range(2):
            for k0 in range(0, 22, 8):
                nk = min(8, 22 - k0)
                bw, wv = wload(wov[:, k0:k0 + nk, hf * 512:(hf + 1) * 512], nk * 512, (nk, 512), d_ws[(l, nm + "o")])
                for b in range(NB):
                    pb = MMB[b]
                    for kk in range(nk):
                        kc = k0 + kk
                        em.op("pe", lambda e, b=b, pb=pb, kk=kk, kc=kc, wv=wv: e.matmul(
                            bank(pb), lhsT=hv[:, kc, b * 128:(b + 1) * 128], rhs=wv[:, kk, :],
                            start=(kc == 0), stop=(kc == 21)),
                            reads=[bw, B_hid], writes=[B_ps[pb]])
                yield
            for b in range(NB):
                pb = MMB[b]
                em.op("dve", lambda e, b=b, hf=hf, pb=pb: e.scalar_tensor_tensor(
                    out=xb[:, b, hf * 512:(hf + 1) * 512], in0=bank(pb), scalar=0.5, op0=ALU.mult,
                    in1=xb[:, b, hf * 512:(hf + 1) * 512], op1=ALU.add),
                    reads=[B_ps[pb], B_xb], writes=[B_xb])
            yield

    def prep_common(grp, l, d, gs, kview_hg):
        ss = grp % NSET
        t_sg, t_g, t_b, t_k, t_A, t_tot, t_tmA = t_sg_l[ss], t_g_l[ss], t_b_l[ss], t_k_l[ss], t_A_l[ss], t_tot_l[ss], t_tmA_l[ss]
        B_sg, B_g, B_b, B_k, B_A, B_tot, B_tmA = B_sg_l[ss], B_g_l[ss], B_b_l[ss], B_k_l[ss], B_A_l[ss], B_tot_l[ss], B_tmA_l[ss]
        bflat = t_b[:, :, :].rearrange("p j t -> p (j t)")
        gflat = t_g[:, :, :].rearrange("p j t -> p (j t)")
        em.op("dve", lambda e: e.tensor_tensor_scan(out=bflat, data0=rmask[:], data1=gflat, initial=0.0,
                                                    op0=ALU.mult, op1=ALU.add),
              reads=[B_g, B_c], writes=[B_b])
        b4 = t_b[:, :, :].rearrange("p j (c t) -> p j c t", t=CH)
        em.op("pool", lambda e: e.tensor_copy(out=t_A[:], in_=b4[:, :, :, CH // 2 - 1]), reads=[B_b], writes=[B_A])
        em.op("pool", lambda e: e.tensor_copy(out=t_tot[:], in_=b4[:, :, :, CH - 1]), reads=[B_b], writes=[B_tot])
        em.op("pool", lambda e: e.tensor_tensor(out=t_tmA[:], in0=t_tot[:], in1=t_A[:], op=ALU.subtract),
              reads=[B_tot, B_A], writes=[B_tmA])
        if d == 1:
            em.op("pool", lambda e: e.tensor_tensor(out=t_b[:], in0=t_b[:], in1=t_g[:], op=ALU.subtract),
                  reads=[B_b, B_g], writes=[B_b])
        em.op("pool", lambda e: e.tensor_tensor(out=b4, in0=b4, in1=t_A[:].unsqueeze(3).to_broadcast([128, 2, NCH, CH]),
                                                op=ALU.subtract),
              reads=[B_b, B_A], writes=[B_b])
        sq, sk = (gs, -gs) if d == 0 else (-gs, gs)
        em.op("act", lambda e: e.activation(out=t_sg[:], in_=t_b[:], func=AF.Exp, scale=sq), reads=[B_b], writes=[B_sg])
        em.op("act", lambda e: e.activation(out=t_g[:], in_=t_b[:], func=AF.Exp, scale=sk), reads=[B_b], writes=[B_g])
        e_in_src, e_u_src = (t_A, t_tmA) if d == 0 else (t_tmA, t_A)
        em.op("act", lambda e: e.activation(out=esc[grp][:, 0], in_=e_in_src[:], func=AF.Exp, scale=gs),
              reads=[B_A, B_tmA], writes=[B_esc[grp]])
        em.op("act", lambda e: e.activation(out=esc[grp][:, 1], in_=t_tot[:], func=AF.Exp, scale=gs),
              reads=[B_tot], writes=[B_esc[grp]])
        em.op("act", lambda e: e.activation(out=esc[grp][:, 2], in_=e_u_src[:], func=AF.Exp, scale=gs),
              reads=[B_A, B_tmA], writes=[B_esc[grp]])
        if grp < 2:
            qv = qh[:, 2 * grp:2 * grp + 2, :]
            em.op("pool", lambda e: e.tensor_tensor(out=QdT[grp][:], in0=qv, in1=t_sg[:], op=ALU.mult),
                  reads=[B_qh, B_sg], writes=[B_Qd[grp]])
            em.op("pool", lambda e: e.tensor_tensor(out=KdT[grp][:], in0=t_k[:], in1=t_g[:], op=ALU.mult),
                  reads=[B_k, B_g], writes=[B_Kd[grp]])
        else:
            for h in range(4):
                j, p0 = h // 2, (h % 2) * 64
                em.op("pool", lambda e, h=h, j=j, p0=p0: e.tensor_tensor(out=Qz[h][p0:p0 + 64, :], in0=qg[p0:p0 + 64, j, :],
                                                                        in1=t_sg[p0:p0 + 64, j, :], op=ALU.mult),
                      reads=[B_qg, B_sg], writes=[B_Qz[h]])
            em.op("pool", lambda e: e.tensor_tensor(out=KdT[2][:], in0=kg[:], in1=t_g[:], op=ALU.mult),
                  reads=[B_kg, B_g], writes=[B_Kd[2]])


    def projections(l, d, want_gates, si, tix):
        wi_v = lambda g: s_wi[l][g].rearrange("p (kc n) -> p kc n", kc=8)
        qkv_parts = [(qh[:, :, :].rearrange("p a t -> p (a t)"), B_qh, 0, 4 * TT),
                     (qg[:, :, :].rearrange("p a t -> p (a t)"), B_qg, 4 * TT, 2 * TT),
                     (kg[:, :, :].rearrange("p a t -> p (a t)"), B_kg, 6 * TT, 2 * TT),
                     (Vt[:, :, :].rearrange("p a t -> p (a t)"), B_Vt, 8 * TT, NB * 1024)]
        if d == 1:
            for k, (view, bb, o0, n) in enumerate(qkv_parts):
                srcq = s_qkv[si][tix][:, o0:o0 + n]
                em.dma(lambda sp, sem, view=view, srcq=srcq: sp.dma_start(out=view, in_=srcq).then_inc(sem, 16), 1,
                       reads=[d_qkv[si][k]], writes=[bb], owner=bb)
        if d == 0:
            bw, wv = wload(wi_v(G_Q), 4096, (8, 512), d_ws[(l, "wi")]); cur_w[0] = bw
            for c in range(4):
                pb = nextbank()
                fmajor_mm(wv, c, pb)
                em.op("act", lambda e, c=c, pb=pb: e.activation(out=qh[:, c, :], in_=bank(pb), func=AF.Copy),
                      reads=[B_ps[pb]], writes=[B_qh])
            bw, wv = wload(wi_v(G_GQK), 4096, (8, 512), d_ws[(l, "wi")]); cur_w[0] = bw
            for c in range(4):
                pb = nextbank()
                fmajor_mm(wv, c, pb)
                if c < 2:
                    em.op("dve", lambda e, c=c, pb=pb: e.tensor_scalar(out=qg[:, c, :], in0=bank(pb), scalar1=0.125, scalar2=None, op0=ALU.mult),
                          reads=[B_ps[pb]], writes=[B_qg])
                else:
                    em.op("dve", lambda e, c=c, pb=pb: e.tensor_copy(out=kg[:, c - 2, :], in_=bank(pb)),
                          reads=[B_ps[pb]], writes=[B_kg])

        def vproj(gi, col):
            bw, wv = wload(wi_v(gi), 4096, (8, 512), d_ws[(l, "wi")])
            for b in range(NB):
                pb = nextbank()
                for kc in range(8):
                    em.op("pe", lambda e, b=b, kc=kc, pb=pb, wv=wv: e.matmul(bank(pb), lhsT=hT[:, kc, b * 128:(b + 1) * 128],
                                                                           rhs=wv[:, kc, :], start=(kc == 0), stop=(kc == 7)),
                          reads=[bw, B_hT], writes=[B_ps[pb]])
                em.op("dve", lambda e, b=b, pb=pb, col=col: e.tensor_copy(out=Vt[:, b, col:col + 512], in_=bank(pb)),
                      reads=[B_ps[pb]], writes=[B_Vt])

        def gproj(gi, u0, fn):
            bw, wv = wload(wi_v(gi), 4096, (8, 512), d_ws[(l, "wi")]); cur_w[0] = bw
            for c in range(4):
                pb = nextbank()
                fmajor_mm(wv, c, pb)
                em.op("act", lambda e, c=c, pb=pb, u0=u0, fn=fn: e.activation(out=gateT[:, u0 + c, :], in_=bank(pb), func=fn),
                      reads=[B_ps[pb]], writes=[B_gate])

        bwz, wvz = wload(wi_v(G_ZF if d == 0 else G_ZB), 4096, (8, 512), d_ws[(l, "wi")])
        for grp in range(2):
            cur_w[0] = bwz
            ss = grp % NSET
            t_sg, t_g, t_k = t_sg_l[ss], t_g_l[ss], t_k_l[ss]
            B_sg, B_g, B_k = B_sg_l[ss], B_g_l[ss], B_k_l[ss]
            for jj in range(2):
                c = grp * 2 + jj
                pb = nextbank()
                fmajor_mm(wvz, c, pb)
                em.op("act", lambda e, jj=jj, pb=pb, t_sg=t_sg: e.activation(out=t_sg[:, jj, :], in_=bank(pb), func=AF.Sigmoid),
                      reads=[B_ps[pb]], writes=[B_sg])
            for jj in range(2):
                col = d * 4 + grp * 2 + jj
                em.op("act", lambda e, jj=jj, col=col, t_g=t_g, t_sg=t_sg: e.activation(out=t_g[:, jj, :], in_=t_sg[:, jj, :], func=AF.Ln,
                                                                    scale=omlT[:, l, col:col + 1], bias=lbT[:, l, col:col + 1]),
                      reads=[B_sg, B_lb], writes=[B_g])
                em.op("dve", lambda e, jj=jj, col=col, t_k=t_k, t_sg=t_sg: e.tensor_scalar(out=t_k[:, jj, :], in0=t_sg[:, jj, :],
                                                                        scalar1=nomlT[:, l, col:col + 1],
                                                                        scalar2=omlT[:, l, col:col + 1],
                                                                        op0=ALU.mult, op1=ALU.add),
                      reads=[B_sg, B_lb], writes=[B_k])
            prep_common(grp, l, d, 1.0, None)
            if d == 0:
                if grp == 0:
                    vproj(G_VH, 0)
                else:
                    vproj(G_VG, 512)
        bw, wv = wload(wi_v(G_GA), 8 * 144, (8, 144), d_ws[(l, "wi")]); cur_w[0] = bw
        ss = 2 % NSET
        t_sg, t_g = t_sg_l[ss], t_g_l[ss]
        B_sg, B_g = B_sg_l[ss], B_g_l[ss]
        pb = nextbank()
        fmajor_mm(wv, 0, pb, ncols=128, col0=16 * d)
        em.op("act", lambda e, pb=pb: e.activation(out=gaT[0:16, :], in_=bank(pb)[0:16, :], func=AF.Copy),
              reads=[B_ps[pb]], writes=[B_gaT])
        for j in range(2):
            pb = nextbank()
            em.op("pe", lambda e, j=j, pb=pb: e.matmul(bank(pb), lhsT=wgb[:, l, d, j * 128:(j + 1) * 128], rhs=gaT[:, :],
                                                        start=True, stop=True),
                  reads=[B_gaT, B_c3], writes=[B_ps[pb]])
            em.op("act", lambda e, j=j, pb=pb, t_sg=t_sg: e.activation(out=t_sg[:, j, :], in_=bank(pb), func=AF.Sigmoid,
                                                            bias=bgT[:, l, d, j:j + 1]),
                  reads=[B_ps[pb], B_c], writes=[B_sg])
        em.op("act", lambda e, t_g=t_g, t_sg=t_sg: e.activation(out=t_g[:], in_=t_sg[:], func=AF.Ln), reads=[B_sg], writes=[B_g])
        prep_common(2, l, d, 1.0 / 16.0, None)
        if d == 0:
            for k, (view, bb, o0, n) in enumerate(qkv_parts):
                dstq = s_qkv[si][tix][:, o0:o0 + n]
                em.dma(lambda sp, sem, view=view, dstq=dstq: sp.dma_start(out=dstq, in_=view).then_inc(sem, 16), 1,
                       reads=[bb], writes=[d_qkv[si][k]], owner=bb)
        if want_gates:
            gproj(G_HG, 0, AF.Sigmoid)
            gproj(G_GR, 4, AF.Silu)
            em.op("pool", lambda e: e.tensor_tensor(out=gateT[:], in0=gateT[:],
                                                    in1=g_hn[:, l, :].unsqueeze(2).to_broadcast([128, 8, TT]), op=ALU.mult),
                  reads=[B_gate, B_c], writes=[B_gate])

    def unit(u):
        if u < 4:
            grp, j = u // 2, u % 2
            return dict(qrhs=lambda t0, n: QdT[grp][:, j, t0:t0 + n], qbuf=B_Qd[grp],
                        klhs=lambda t0: KdT[grp][:, j, t0:t0 + 128], kbuf=B_Kd[grp],
                        kc_cols=(u * 128, u * 128 + 128), si=u, p0=0, pn=128, vcol=u * 128, grp=grp, j=j)
        h = u - 4
        j, p0 = h // 2, (h % 2) * 64
        return dict(qrhs=lambda t0, n: Qz[h][:, t0:t0 + n], qbuf=B_Qz[h],
                    klhs=lambda t0: KdT[2][:, j, t0:t0 + 128], kbuf=B_Kd[2],
                    kc_cols=(512 + j * 128, 512 + j * 128 + 128), si=4 + j, p0=p0, pn=64, vcol=512 + h * 128, grp=2, j=j)

    units = [unit(u) for u in range(8)]
    AT_B, O_BK, U_BS, TR_B = 0, 1, (2, 3), 3

    def scan_half(l, d, b, hb, first_chunk_of_seq, evac_fn):
        tb = b * 128
        us = list(range(hb * 4, hb * 4 + 4))
        corder = (0, 1) if d == 0 else (1, 0)

        def uinfo(u, ci):
            un = units[u]
            return (un, un["p0"], un["pn"], un["si"], esc[un["grp"]], B_S[un["si"]], B_Sp[un["si"]],
                    bank(U_BS[ci])[:, (u % 4) * 128:(u % 4) * 128 + 128])

        def emit_sp(ci):
            c = corder[ci]
            cl = (tb // CH) + c
            for u in us:
                un, p0, pn, sidx, e3, bS, bSp, ureg = uinfo(u, ci)
                em.op("act", lambda e, p0=p0, pn=pn, sidx=sidx, e3=e3, un=un, cl=cl: e.activation(
                    out=Sp[p0:p0 + pn, sidx, :], in_=S[p0:p0 + pn, sidx, :], func=AF.Copy,
                    scale=e3[p0:p0 + pn, 0, un["j"], cl:cl + 1]),
                    reads=[bS, B_esc[un["grp"]]], writes=[bSp])

        def emit_state_mm(ci):
            c = corder[ci]
            for u in us:
                un, p0, pn, sidx, e3, bS, bSp, ureg = uinfo(u, ci)
                em.op("pe", lambda e, u=u, un=un, sidx=sidx, c=c: e.matmul(
                    bank(O_BK)[:, (u % 4) * 128 + c * CH:(u % 4) * 128 + c * CH + CH],
                    lhsT=Sp[:, sidx, :], rhs=un["qrhs"](tb + c * CH, CH), start=False, stop=False, skip_group_check=True),
                    reads=[bSp, un["qbuf"]], writes=[B_ps[O_BK]])

        def emit_u_and_update(ci, first):
            c = corder[ci]
            cl = (tb // CH) + c
            pbu = U_BS[ci]
            for u in us:
                un, p0, pn, sidx, e3, bS, bSp, ureg = uinfo(u, ci)
                c0, c1 = un["kc_cols"]
                em.op("pe", lambda e, un=un, ureg=ureg, c=c, c0=c0, c1=c1: e.matmul(
                    ureg, lhsT=Kc[c][:, c0:c1], rhs=Vt[:, b, un["vcol"]:un["vcol"] + 128], start=True, stop=True),
                    reads=[B_Kc[c][hb], B_Vt], writes=[B_ps[pbu]])
            for u in us:
                un, p0, pn, sidx, e3, bS, bSp, ureg = uinfo(u, ci)
                sl = S[p0:p0 + pn, sidx, :]
                if first:
                    em.op("dve", lambda e, sl=sl, ureg=ureg, p0=p0, pn=pn, e3=e3, un=un, cl=cl: e.tensor_scalar(
                        out=sl, in0=ureg[p0:p0 + pn, :], scalar1=e3[p0:p0 + pn, 2, un["j"], cl:cl + 1], scalar2=None, op0=ALU.mult),
                        reads=[B_ps[pbu], B_esc[un["grp"]]], writes=[bS])
                else:
                    em.op("pool", lambda e, sl=sl, p0=p0, pn=pn, e3=e3, un=un, cl=cl: e.tensor_scalar(
                        out=sl, in0=sl, scalar1=e3[p0:p0 + pn, 1, un["j"], cl:cl + 1], scalar2=1.0, op0=ALU.mult, op1=ALU.mult),
                        reads=[bS, B_esc[un["grp"]]], writes=[bS])
                    em.op("dve", lambda e, sl=sl, ureg=ureg, p0=p0, pn=pn, e3=e3, un=un, cl=cl: e.scalar_tensor_tensor(
                        out=sl, in0=ureg[p0:p0 + pn, :], scalar=e3[p0:p0 + pn, 2, un["j"], cl:cl + 1], op0=ALU.mult,
                        in1=sl, op1=ALU.add),
                        reads=[B_ps[pbu], bS, B_esc[un["grp"]]], writes=[bS])

        for u in us:
            un = units[u]
            em.op("pe", lambda e, u=u, un=un: e.matmul(bank(AT_B)[:, (u % 4) * 128:(u % 4) * 128 + 128],
                                                       lhsT=un["klhs"](tb), rhs=un["qrhs"](tb, 128), start=True, stop=True),
                  reads=[un["kbuf"], un["qbuf"]], writes=[B_ps[AT_B]])
        tr = bank16(TR_B)
        tlist = [0, 1, 2, 3] if hb == 0 else [4, 5]
        for i in tlist:
            src = KdT[i // 2][:, i % 2, tb:tb + 128]
            em.op("pe", lambda e, i=i, src=src: e.transpose(out=tr[:, i * 128:(i + 1) * 128], in_=src, identity=ident[:]),
                  reads=[B_Kd[i // 2], B_c2], writes=[B_ps[TR_B]])
        k0c, k1c = (0, 512) if hb == 0 else (512, 768)
        for c in range(2):
            r0 = c * CH
            em.op("act", lambda e, c=c, r0=r0: e.activation(out=Kc[c][r0:r0 + CH, k0c:k1c], in_=tr[r0:r0 + CH, k0c:k1c], func=AF.Copy),
                  reads=[B_ps[TR_B]], writes=[B_Kc[c][hb]])
        for c in range(2):
            r0 = c * CH
            src = bank(AT_B)[r0:r0 + CH, :].rearrange("p (h t) -> p h t", h=4)[:, :, r0:r0 + CH]
            dst = ATs[r0:r0 + CH, hb * 4:hb * 4 + 4, r0:r0 + CH]
            mk = mask[r0:r0 + CH, d, :].unsqueeze(1).to_broadcast([CH, 4, CH])
            em.op("dve", lambda e, src=src, dst=dst, mk=mk: e.tensor_tensor(out=dst, in0=src, in1=mk, op=ALU.mult),
                  reads=[B_ps[AT_B], B_c], writes=[B_ATs[hb]])
        if not first_chunk_of_seq:
            emit_sp(0)
        yield
        for u in us:
            un = units[u]
            em.op("pe", lambda e, u=u, un=un: e.matmul(bank(O_BK)[:, (u % 4) * 128:(u % 4) * 128 + 128],
                                                       lhsT=Vt[:, b, un["vcol"]:un["vcol"] + 128], rhs=ATs[:, u, :],
                                                       start=(u % 4 == 0), stop=False, skip_group_check=True),
                  reads=[B_Vt, B_ATs[hb]], writes=[B_ps[O_BK]])
        if not first_chunk_of_seq:
            emit_state_mm(0)
        emit_u_and_update(0, first_chunk_of_seq)
        emit_sp(1)
        yield
        emit_state_mm(1)
        emit_u_and_update(1, False)
        evac_fn()
        yield

    def flat(ap):
        return ap.rearrange("p u t -> p (u t)")

    def scan_tile(si, l, d, t0, first_tile):
        border = list(range(NB)) if d == 0 else list(range(NB - 1, -1, -1))
        for bi, b in enumerate(border):
            gb = (t0 // 128) + b
            if d == 1:
                srcv2 = s_of[si][gb]
                em.dma(lambda sp, sem, srcv2=srcv2: sp.dma_start(out=flat(ofl[:]), in_=srcv2).then_inc(sem, 16),
                       1, reads=[d_of[si]], writes=[B_ofl], owner=B_ofl)
            for hb in range(2):
                oh = flat(osb[:, hb * 4:hb * 4 + 4, :])

                def evac_fn(oh=oh, hb=hb):
                    if d == 0:
                        if hb == 0:
                            em.op("act", lambda e: e.activation(out=oh, in_=bank(O_BK), func=AF.Copy),
                                  reads=[B_ps[O_BK]], writes=[B_osb])
                        else:
                            em.op("dve", lambda e: e.tensor_copy(out=oh, in_=bank(O_BK)), reads=[B_ps[O_BK]], writes=[B_osb])
                    else:
                        em.op("dve", lambda e: e.tensor_tensor(out=oh, in0=bank(O_BK), in1=flat(ofl[:, hb * 4:hb * 4 + 4, :]), op=ALU.add),
                              reads=[B_ps[O_BK], B_ofl], writes=[B_osb])
                yield from scan_half(l, d, b, hb, (first_tile and bi == 0), evac_fn)
            if d == 0:
                dstv = s_of[si][gb]
                em.dma(lambda sp, sem, dstv=dstv: sp.dma_start(out=dstv, in_=flat(osb[:])).then_inc(sem, 16),
                       1, reads=[B_osb], writes=[d_of[si]], owner=B_osb)
            else:
                em.op("act", lambda e: e.activation(out=sqb[:], in_=osb[:], func=AF.Square), reads=[B_osb], writes=[B_sqb])
                for hb in range(2):
                    em.op("pe", lambda e, hb=hb: e.matmul(bank(AT_B), lhsT=ones_b[:], rhs=flat(sqb[:, hb * 4:hb * 4 + 4, :]),
                                                          start=True, stop=True),
                          reads=[B_sqb, B_c2], writes=[B_ps[AT_B]])
                    em.op("act", lambda e, hb=hb: e.activation(out=flat(rsb[:, hb * 4:hb * 4 + 4, :]), in_=bank(AT_B), func=AF.Ln,
                                                               scale=1.0 / 128.0, bias=epsc[:]),
                          reads=[B_ps[AT_B], B_c2], writes=[B_rsb])
                em.op("act", lambda e: e.activation(out=rsb[:], in_=rsb[:], func=AF.Exp, scale=-0.5), reads=[B_rsb], writes=[B_rsb])
                em.op("pool", lambda e: e.tensor_tensor(out=osb[:], in0=osb[:], in1=rsb[:], op=ALU.mult),
                      reads=[B_osb, B_rsb], writes=[B_osb])
                em.op("pool", lambda e, b=b: e.tensor_tensor(out=onT[:, :, b * 128:(b + 1) * 128], in0=osb[:],
                                                             in1=gateT[:, :, b * 128:(b + 1) * 128], op=ALU.mult),
                      reads=[B_osb, B_gate], writes=[B_onT])
            yield

    def interleave(*gs):
        gens = [g for g in gs if g is not None]
        while gens:
            for g in list(gens):
                try:
                    next(g)
                except StopIteration:
                    gens.remove(g)

    def run(g):
        for _ in g:
            pass

    def xview(ap2d, t0):
        return ap2d[t0:t0 + TT, :].rearrange("(b p) d -> p b d", p=128)

    def preload_fwd(si, l, t0, x_in):
        xb, Bx = xbs[0], B_xbs[0]
        src = x_in if l == 0 else s_x[si]
        srcv = xview(src, t0)
        em.dma(lambda sp, sem: sp.dma_start(out=xb[:], in_=srcv).then_inc(sem, 16), 1,
               reads=([] if l == 0 else [d_x[si]]), writes=[Bx], owner=Bx)

    def ffn_fwd_stage(si, l, t0):
        xb, Bx = xbs[0], B_xbs[0]
        yield from norm_to_hT(g_n1, l, xb, Bx)
        yield from ffn("f1", l, xb, Bx)
        dstv = xview(s_x[si], t0)
        em.dma(lambda sp, sem: sp.dma_start(out=dstv, in_=xb[:]).then_inc(sem, 16), 1,
               reads=[Bx], writes=[d_x[si]], owner=Bx)
        yield
        yield from norm_to_hT(g_nm, l, xb, Bx)

    def preload_bwd(si, t0, k):
        xb, Bx = xbs[k], B_xbs[k]
        srcv = xview(s_x[si], t0)
        em.dma(lambda sp, sem: sp.dma_start(out=xb[:], in_=srcv).then_inc(sem, 16), 1,
               reads=[d_x[si]], writes=[Bx], owner=Bx)

    def chain(*gens):
        for g in gens:
            if g is not None:
                yield from g

    def gen_call(fn, *args):
        fn(*args)
        yield

    def ffn_bwd_stage(si, l, t0, k, y_out):
        xb, Bx = xbs[k], B_xbs[k]
        yield from norm_to_hT(g_n2, l, xb, Bx)
        yield from ffn("f2", l, xb, Bx)
        if l < n_layers - 1:
            dstv = xview(s_x[si], t0)
            em.dma(lambda sp, sem: sp.dma_start(out=dstv, in_=xb[:]).then_inc(sem, 16), 1,
                   reads=[Bx], writes=[d_x[si]], owner=Bx)
        else:
            for b in range(NB):
                junk = hid[:, b * 1024:b * 1024 + 1024]
                em.op("act", lambda e, b=b, junk=junk: e.activation(out=junk, in_=xb[:, b, :], func=AF.Square,
                                                                    accum_out=nstat[:, b:b + 1]),
                      reads=[Bx], writes=[B_hid, B_nstat])
            em.op("act", lambda e: e.activation(out=nrstd[:, 0:NB], in_=nstat[:, 0:NB], func=AF.Ln, scale=1.0 / D, bias=epsc[:]),
                  reads=[B_nstat, B_c2], writes=[B_nrstd])
            em.op("act", lambda e: e.activation(out=nrstd[:, 0:NB], in_=nrstd[:, 0:NB], func=AF.Exp, scale=-0.5),
                  reads=[B_nrstd], writes=[B_nrstd])
            for b in range(NB):
                em.op("dve", lambda e, b=b: e.scalar_tensor_tensor(out=xb[:, b, :], in0=xb[:, b, :], scalar=nrstd[:, b:b + 1],
                                                                    op0=ALU.mult, in1=g_fin[:], op1=ALU.mult),
                      reads=[Bx, B_nrstd, B_c], writes=[Bx])
            dstv = xview(y_out, t0)
            em.dma(lambda sp, sem: sp.dma_start(out=dstv, in_=xb[:]).then_inc(sem, 16), 1,
                   reads=[Bx], writes=[], owner=Bx)
        yield

    for si, (T, x_in, y_out) in enumerate(seqs):
        ntile = T // TT
        for l in range(n_layers):
            lazy = lazy_convert() if (si == 0 and l == 0) else None
            preload_fwd(si, l, 0, x_in)
            interleave(ffn_fwd_stage(si, l, 0), lazy)
            for ti in range(ntile):
                has_next = ti + 1 < ntile
                projections(l, 0, False, si, ti)
                _chk("proj")
                if has_next:
                    preload_fwd(si, l, (ti + 1) * TT, x_in)
                nxt = ffn_fwd_stage(si, l, (ti + 1) * TT) if has_next else None
                interleave(scan_tile(si, l, 0, ti * TT, first_tile=(ti == 0)), nxt, lazy)
                _chk("fwdpass")
            if lazy is not None:
                run(lazy)
            prev = None
            border_t = list(range(ntile - 1, -1, -1))
            preload_bwd(si, border_t[0] * TT, 0)
            run(norm_to_hT(g_nm, l, xbs[0], B_xbs[0]))
            for oi, ti in enumerate(border_t):
                k = oi % 2
                xb, Bx = xbs[k], B_xbs[k]
                has_next = oi + 1 < ntile
                projections(l, 1, True, si, ti)
                pg = ffn_bwd_stage(si, l, prev[0] * TT, prev[1], y_out) if prev is not None else None
                pl = gen_call(preload_bwd, si, border_t[oi + 1] * TT, 1 - k) if has_next else None
                nn = norm_to_hT(g_nm, l, xbs[1 - k], B_xbs[1 - k]) if has_next else None
                interleave(scan_tile(si, l, 1, ti * TT, first_tile=(oi == 0)), chain(pg, pl, nn))
                wov = s_wo[l].rearrange("(kc p) n -> p kc n", p=128)
                slots = [wload(wov[:, k0:k0 + 4, :], 4096, (4, 1024), d_ws[(l, "wo")]) for k0 in (0, 4)]
                for b in range(NB):
                    for hf in range(2):
                        pb = nextbank()
                        for kc in range(8):
                            bw, wv = slots[kc // 4]
                            em.op("pe", lambda e, b=b, hf=hf, pb=pb, kc=kc, wv=wv: e.matmul(
                                bank(pb), lhsT=onT[:, kc, b * 128:(b + 1) * 128], rhs=wv[:, kc % 4, hf * 512:(hf + 1) * 512],
                                start=(kc == 0), stop=(kc == 7)),
                                reads=[bw, B_onT], writes=[B_ps[pb]])
                        em.op("dve", lambda e, b=b, hf=hf, pb=pb, xb=xb: e.tensor_tensor(
                            out=xb[:, b, hf * 512:(hf + 1) * 512], in0=bank(pb), in1=xb[:, b, hf * 512:(hf + 1) * 512], op=ALU.add),
                            reads=[B_ps[pb], Bx], writes=[Bx])
                prev = (ti, k)
            run(ffn_bwd_stage(si, l, prev[0] * TT, prev[1], y_out))


def make_consts():
    ident = np.eye(128, dtype=np.float32)
    p = np.arange(128)[:, None] % CH
    t = np.arange(CH)[None, :]
    mask = np.stack([(p <= t), (p >= t)], axis=1).astype(np.float32)
    rm = np.ones((2 * TT,), np.float32)
    rm[::CH] = 0.0
    import ml_dtypes
    rmask = np.broadcast_to(rm, (128, 2 * TT)).astype(ml_dtypes.bfloat16)
    return {"c_ident": ident, "c_mask": mask, "c_rmask": rmask}


_W_NAMES = ["lower_bounds", "ffn1_norm", "ffn1_w_in", "ffn1_w_out", "mix_norm", "w_in", "gla_w_gate", "gla_b_gate",
            "hg_head_norm", "gla_head_norm", "w_out", "ffn2_norm", "ffn2_w_in", "ffn2_w_out", "final_norm"]


def kernel(**inputs):
    n = 8
    xp = np.ascontiguousarray(inputs["x_prompt"], dtype=np.float32)
    xs = np.ascontiguousarray(inputs["x_sample"], dtype=np.float32)
    BP, TP, _ = xp.shape
    BS, TS, _ = xs.shape
    npc, nsc = BP // n, BS // n
    nc = build_program((npc, TP), (nsc, TS))
    consts = make_consts()
    shared = {k: np.ascontiguousarray(inputs[k], dtype=np.float32) for k in _W_NAMES}
    in_maps = []
    for c in range(n):
        m = dict(shared)
        m.update(consts)
        m["xp"] = xp[c * npc:(c + 1) * npc]
        m["xs"] = xs[c * nsc:(c + 1) * nsc]
        in_maps.append(m)
    res = run_bass_kernel_spmd(nc, in_maps, core_ids=list(range(n)))
    yp = np.concatenate([r["yp"] for r in res.results], axis=0).astype(np.float32)
    ys = np.concatenate([r["ys"] for r in res.results], axis=0).astype(np.float32)
    return (yp, ys)
```

```python
import numpy as np
from contextlib import ExitStack
import concourse.bass as bass
import concourse.mybir as mybir
from concourse.bass_utils import run_bass_kernel_spmd

F32 = mybir.dt.float32
BF16 = mybir.dt.bfloat16
AF = mybir.ActivationFunctionType
ALU = mybir.AluOpType

D = 1024
DFF = 2816
DIN = 4128
L = 2
TT = 512
NB = TT // 128
CH = 64
NCH = TT // CH
EPS = 1e-6
NSLOT = 4
WI_GROUPS = [(0, 512), (512, 512), (1024, 512), (1536, 512), (2048, 512), (2560, 512), (3072, 512),
             (3584, 144), (3616, 512)]
G_Q, G_ZF, G_ZB, G_VH, G_HG, G_GQK, G_VG, G_GA, G_GR = range(9)


class Buf:
    __slots__ = ("name", "w", "r", "sem", "cnt")

    def __init__(self, name):
        self.name = name
        self.w = None
        self.r = []
        self.sem = None
        self.cnt = 0


class Op:
    __slots__ = ("eng", "fn", "deps", "dma", "sem", "val", "sig", "users", "ndma")

    def __init__(self, eng, fn):
        self.eng = eng
        self.fn = fn
        self.deps = []
        self.dma = False
        self.sem = None
        self.val = 0
        self.sig = 0
        self.users = 0
        self.ndma = 0


ENGS = ("sp", "act", "pool", "pe", "dve")


class Em:
    def __init__(self, nc, es):
        self.nc = nc
        self.es = es
        self.ops = {e: [] for e in ENGS}
        self.esem = {e: es.enter_context(nc.semaphore("s_" + e)) for e in ENGS if e != "sp"}
        self.nsem = 0
        self.bar = {e: [] for e in ENGS}
        self.all_dma = []

    def _dep(self, op, p):
        if p is None or p is op:
            return
        if p.eng == op.eng and not p.dma:
            if op.eng == "pe" or op.eng == "sp":
                return
        op.deps.append(p)

    def op(self, eng, fn, reads=(), writes=(), same_raw_only=False):
        o = Op(eng, fn)
        for b in reads:
            self._dep(o, b.w)
        for b in writes:
            if not (b.w is not None and b.w.eng == eng and not b.w.dma and same_raw_only):
                self._dep(o, b.w)
            for r in b.r:
                if r.eng == eng and not r.dma and same_raw_only:
                    continue
                self._dep(o, r)
        for p in self.bar[eng]:
            self._dep(o, p)
        self.bar[eng] = []
        for b in reads:
            b.r.append(o)
        for b in writes:
            b.w = o
            b.r = []
        self.ops[eng].append(o)
        return o

    def dma(self, fn, ndma, reads=(), writes=(), owner=None):
        o = self.op("sp", fn, reads, writes)
        o.dma = True
        o.ndma = ndma
        if owner.sem is None:
            owner.sem = self.es.enter_context(self.nc.semaphore("d%d" % self.nsem))
            self.nsem += 1
        owner.cnt += 16 * ndma
        o.sem = owner.sem
        o.val = owner.cnt
        self.all_dma.append(o)
        return o

    @staticmethod
    def inherit(child, parent):
        child.w = parent.w
        child.r = list(parent.r)

    @staticmethod
    def merge(parent, children):
        rs = list(parent.r)
        for c in children:
            rs += list(c.r)
            if c.w is not None:
                rs.append(c.w)
        parent.r = rs

    def barrier(self):
        lasts = [self.ops[e][-1] for e in ENGS if self.ops[e] and e != "sp"]
        lat = {}
        for o in self.all_dma:
            lat[id(o.sem)] = o
        for e in ENGS:
            self.bar[e] = lasts + list(lat.values())

    def emit(self):
        nc = self.nc
        for e in ENGS:
            for o in self.ops[e]:
                for p in o.deps:
                    p.users += 1
        for e in ENGS:
            if e == "sp":
                continue
            k = 0
            for o in self.ops[e]:
                if o.users > 0:
                    k += 1
                    o.sig = k
                    o.sem = self.esem[e]
                    o.val = k
        engobj = {"sp": nc.sync, "act": nc.scalar, "pool": nc.gpsimd, "pe": nc.tensor, "dve": nc.vector}
        finals = {}
        for o in self.all_dma:
            finals[id(o.sem)] = o

        def run(e, eng):
            known = {}
            for o in self.ops[e]:
                need = {}
                for p in o.deps:
                    key = id(p.sem)
                    if known.get(key, 0) >= p.val:
                        continue
                    if key not in need or need[key][1] < p.val:
                        need[key] = (p.sem, p.val)
                for key, (sem, val) in need.items():
                    eng.wait_ge(sem, val)
                    known[key] = val
                if o.dma:
                    o.fn(eng, o.sem)
                else:
                    ins = o.fn(eng)
                    if o.users > 0:
                        ins.then_inc(o.sem, 1)
            if e == "sp":
                for o in finals.values():
                    if known.get(id(o.sem), 0) < o.val:
                        eng.wait_ge(o.sem, o.val)

        with nc.Block() as block:
            @block.sync
            def _(eng):
                run("sp", eng)

            @block.scalar
            def _(eng):
                run("act", eng)

            @block.gpsimd
            def _(eng):
                run("pool", eng)

            @block.tensor
            def _(eng):
                run("pe", eng)

            @block.vector
            def _(eng):
                run("dve", eng)


class _Stop(Exception):
    pass


DEBUG_STOP = [None]


def _chk(tag):
    if DEBUG_STOP[0] == tag:
        raise _Stop()


def build_program(seq_lens_p, seq_lens_s, n_layers=L):
    nc = bass.Bass("TRN2", target_bir_lowering=False)
    es = ExitStack()
    em = Em(nc, es)
    try:
        _build(nc, es, em, seq_lens_p, seq_lens_s, n_layers)
    except _Stop:
        pass
    em.emit()
    es.close()
    return nc


def _build(nc, es, em, seq_lens_p, seq_lens_s, n_layers=L):
    NP, TP = seq_lens_p
    NS, TS = seq_lens_s

    def din(name, shape, dt=F32):
        return nc.dram_tensor(name, list(shape), dt, kind="ExternalInput").ap()

    def dout(name, shape, dt=F32):
        return nc.dram_tensor(name, list(shape), dt, kind="ExternalOutput").ap()

    def dscr(name, shape, dt):
        return nc.dram_tensor(name, list(shape), dt, kind="Internal").ap()

    xp = din("xp", [NP, TP, D])
    xsm = din("xs", [NS, TS, D])
    yp = dout("yp", [NP, TP, D])
    ysm = dout("ys", [NS, TS, D])
    lower_bounds = din("lower_bounds", [L, 1024])
    ffn1_norm = din("ffn1_norm", [L, D])
    ffn1_w_in = din("ffn1_w_in", [L, D, 2 * DFF])
    ffn1_w_out = din("ffn1_w_out", [L, DFF, D])
    mix_norm = din("mix_norm", [L, D])
    w_in = din("w_in", [L, D, DIN])
    gla_w_gate = din("gla_w_gate", [L, 2, 16, 256])
    gla_b_gate = din("gla_b_gate", [L, 2, 256])
    hg_head_norm = din("hg_head_norm", [L, 512])
    gla_head_norm = din("gla_head_norm", [L, 512])
    w_out = din("w_out", [L, D, D])
    ffn2_norm = din("ffn2_norm", [L, D])
    ffn2_w_in = din("ffn2_w_in", [L, D, 2 * DFF])
    ffn2_w_out = din("ffn2_w_out", [L, DFF, D])
    final_norm = din("final_norm", [D])
    c_ident = din("c_ident", [128, 128])
    c_mask = din("c_mask", [128, 2, CH])
    c_rmask = din("c_rmask", [128, 2 * TT], BF16)

    seqs = []
    for i in range(NP):
        seqs.append((TP, xp[i], yp[i]))
    for i in range(NS):
        seqs.append((TS, xsm[i], ysm[i]))

    NG1 = (2 * DFF) // 512
    s_win = {}
    for nm in ("f1", "f2"):
        s_win[nm] = [dscr("sw_%s_%d" % (nm, l), [NG1, 128, 8 * 512], BF16) for l in range(n_layers)]
    s_wi = [[dscr("sw_wi_%d_%d" % (l, g), [128, 8 * WI_GROUPS[g][1]], BF16) for g in range(9)] for l in range(n_layers)]
    s_wout = {}
    for nm in ("f1", "f2"):
        s_wout[nm] = [dscr("so_%s_%d" % (nm, l), [DFF, D], BF16) for l in range(n_layers)]
    s_wo = [dscr("so_wo_%d" % l, [D, D], BF16) for l in range(n_layers)]
    s_x = [dscr("sx_%d" % i, [T, D], F32) for i, (T, _, _) in enumerate(seqs)]
    s_of = [dscr("sof_%d" % i, [T // 128, 128, 1024], F32) for i, (T, _, _) in enumerate(seqs)]
    QKV_N = 4 * TT + 2 * TT + 2 * TT + NB * 1024
    s_qkv = [dscr("sqkv_%d" % i, [T // TT, 128, QKV_N], BF16) for i, (T, _, _) in enumerate(seqs)]
    d_qkv = [[Buf("dqkv%d_%d" % (i, k)) for k in range(4)] for i in range(len(seqs))]
    d_x = [Buf("dx%d" % i) for i in range(len(seqs))]
    d_of = [Buf("dof%d" % i) for i in range(len(seqs))]
    d_w = None

    def sb(name, shape, dt):
        return es.enter_context(nc.sbuf_tensor(name, list(shape), dt))

    xbs = [sb("xb%d" % i, [128, NB, D], F32) for i in range(2)]
    B_xbs = [Buf("xb%d" % i) for i in range(2)]
    hid = sb("hid", [128, 22 * TT], BF16); B_hid = Buf("hid")
    hT = sb("hT", [128, 8, TT], BF16); B_hT = Buf("hT")
    wring = [sb("wr%d" % i, [128, 4096], BF16) for i in range(NSLOT)]
    B_wr = [Buf("wr%d" % i) for i in range(NSLOT)]
    cv32 = [xbs[i][:, :, :].rearrange("p b d -> p (b d)") for i in range(2)]
    B_cv32 = B_xbs
    NSET = 2
    t_sg_l = [sb("t_sg%d" % i, [128, 2, TT], F32) for i in range(NSET)]; B_sg_l = [Buf("t_sg%d" % i) for i in range(NSET)]
    t_g_l = [sb("t_g%d" % i, [128, 2, TT], F32) for i in range(NSET)]; B_g_l = [Buf("t_g%d" % i) for i in range(NSET)]
    t_b_l = [sb("t_b%d" % i, [128, 2, TT], F32) for i in range(NSET)]; B_b_l = [Buf("t_b%d" % i) for i in range(NSET)]
    t_k_l = [sb("t_k%d" % i, [128, 2, TT], BF16) for i in range(NSET)]; B_k_l = [Buf("t_k%d" % i) for i in range(NSET)]
    qh = sb("qh", [128, 4, TT], BF16); B_qh = Buf("qh")
    qg = sb("qg", [128, 2, TT], BF16); B_qg = Buf("qg")
    kg = sb("kg", [128, 2, TT], BF16); B_kg = Buf("kg")
    gaT = sb("gaT", [128, TT], BF16); B_gaT = Buf("gaT")
    QdT = [sb("QdT%d" % g, [128, 2, TT], BF16) for g in range(3)]; B_Qd = [Buf("Qd%d" % g) for g in range(3)]
    KdT = [sb("KdT%d" % g, [128, 2, TT], BF16) for g in range(3)]; B_Kd = [Buf("Kd%d" % g) for g in range(3)]
    Qz = [sb("Qz%d" % h, [128, TT], BF16) for h in range(4)]; B_Qz = [Buf("Qz%d" % h) for h in range(4)]
    t_A_l = [sb("t_A%d" % i, [128, 2, NCH], F32) for i in range(NSET)]; B_A_l = [Buf("t_A%d" % i) for i in range(NSET)]
    t_tot_l = [sb("t_tot%d" % i, [128, 2, NCH], F32) for i in range(NSET)]; B_tot_l = [Buf("t_tot%d" % i) for i in range(NSET)]
    t_tmA_l = [sb("t_tmA%d" % i, [128, 2, NCH], F32) for i in range(NSET)]; B_tmA_l = [Buf("t_tmA%d" % i) for i in range(NSET)]
    esc = [sb("esc%d" % g, [128, 3, 2, NCH], F32) for g in range(3)]; B_esc = [Buf("esc%d" % g) for g in range(3)]
    Vt = sb("Vt", [128, NB, 1024], BF16); B_Vt = Buf("Vt")
    Kc = [sb("Kc%d" % c, [128, 768], BF16) for c in range(2)]
    B_Kc = [[Buf("Kc%d_%d" % (c, h)) for h in range(2)] for c in range(2)]
    ATs = sb("ATs", [128, 8, 128], BF16); B_ATs = [Buf("ATs0"), Buf("ATs1")]
    S = sb("S", [128, 6, 128], F32); B_S = [Buf("S%d" % i) for i in range(6)]
    Sp = sb("Sp", [128, 6, 128], BF16); B_Sp = [Buf("Sp%d" % i) for i in range(6)]
    osb = sb("osb", [128, 8, 128], F32); B_osb = Buf("osb")
    ofl = sb("ofl", [128, 8, 128], F32); B_ofl = Buf("ofl")
    sqb = sb("sqb", [128, 8, 128], BF16); B_sqb = Buf("sqb")
    rsb = sb("rsb", [128, 8, 128], F32); B_rsb = Buf("rsb")
    gateT = sb("gateT", [128, 8, TT], BF16); B_gate = Buf("gateT")
    onT = sb("onT", [128, 8, TT], BF16); B_onT = Buf("onT")
    nstat = sb("nstat", [128, 8], F32); B_nstat = Buf("nstat")
    nrstd = sb("nrstd", [128, 8], F32); B_nrstd = Buf("nrstd")
    B_c = Buf("consts")
    ident32 = sb("ident32", [128, 128], F32)
    ident = sb("ident", [128, 128], BF16)
    ones_b = sb("ones_b", [128, 128], BF16)
    mask = sb("mask", [128, 2, CH], F32)
    rmask = sb("rmask", [128, 2 * TT], BF16)
    epsc = sb("epsc", [128, 1], F32)
    g_n1 = sb("g_n1", [128, L, 8], F32)
    g_nm = sb("g_nm", [128, L, 8], F32)
    g_n2 = sb("g_n2", [128, L, 8], F32)
    g_hn = sb("g_hn", [128, L, 8], F32)
    g_fin = sb("g_fin", [128, D], F32)
    lbraw = sb("lbraw", [128, L, 8], F32)
    lbT = sb("lbT", [128, L, 8], F32)
    omlT = sb("omlT", [128, L, 8], F32)
    nomlT = sb("nomlT", [128, L, 8], F32)
    bgT = sb("bgT", [128, L, 2, 2], F32)
    wg32 = xbs[1][0:16, 0, :].rearrange("p (l d c) -> p l d c", l=L, d=2)
    wgb = sb("wgb", [128, L, 2, 256], BF16)
    psum = es.enter_context(nc.psum_tensor("psum", [128, 8 * 512], F32))
    B_ps = [Buf("ps%d" % i) for i in range(8)]

    def bank(i):
        return psum[:, i * 512:(i + 1) * 512]

    def bank16(i):
        return psum[:, i * 512:(i + 1) * 512].bitcast(BF16)

    def load_consts(sp, sem):
        def ld(out, in_):
            sp.dma_start(out=out, in_=in_).then_inc(sem, 16)
        with nc.allow_non_contiguous_dma(reason="tiny param loads"):
            ld(ident32[:], c_ident[:, :])
            ld(mask[:], c_mask[:, :, :])
            ld(rmask[:], c_rmask[:, :])
            ld(g_n1[:], ffn1_norm.rearrange("l (j p) -> p l j", p=128))
            ld(g_nm[:], mix_norm.rearrange("l (j p) -> p l j", p=128))
            ld(g_n2[:], ffn2_norm.rearrange("l (j p) -> p l j", p=128))
            for l_ in range(L):
                ld(g_hn[:, l_, 0:4], hg_head_norm[l_].rearrange("(j p) -> p j", p=128))
                ld(g_hn[:, l_, 4:8], gla_head_norm[l_].rearrange("(j p) -> p j", p=128))
            ld(g_fin[:], final_norm.partition_broadcast(128))
            ld(lbraw[:], lower_bounds.rearrange("l (j p) -> p l j", p=128))
            for l_ in range(L):
                for d_ in range(2):
                    ld(bgT[:, l_, d_, :], gla_b_gate[l_, d_].rearrange("(j p) -> p j", p=128))
            ld(wg32, gla_w_gate.rearrange("l d r c -> r l d c"))
    em.dma(load_consts, 9 + 2 * L + 2 * L, writes=[B_c, B_xbs[1]], owner=B_c)
    B_c2 = Buf("consts2")
    em.op("dve", lambda e: e.tensor_copy(out=ident[:], in_=ident32[:]), reads=[B_c], writes=[B_c2])
    em.op("dve", lambda e: e.memset(ones_b[:], 1.0), writes=[B_c2])
    em.op("dve", lambda e: e.memset(epsc[:], EPS), writes=[B_c2])
    em.op("dve", lambda e: e.memset(wgb[:], 0.0), writes=[B_c2])
    B_c3 = Buf("consts3")
    em.op("dve", lambda e: e.tensor_copy(out=wgb[0:16], in_=wg32), reads=[B_c, B_c2, B_xbs[1]], writes=[B_c3])
    B_lb = Buf("lb")
    tmpl = sb("tmpl", [128, 8], F32)
    em.op("dve", lambda e: e.tensor_tensor(out=tmpl[:], in0=lbraw[:, 0, :], in1=lbraw[:, 1, :], op=ALU.subtract),
          reads=[B_c], writes=[B_lb])
    em.op("act", lambda e: e.activation(out=tmpl[:], in_=tmpl[:], func=AF.Exp), reads=[B_lb], writes=[B_lb])
    em.op("dve", lambda e: e.tensor_scalar(out=tmpl[:], in0=tmpl[:], scalar1=1.0, scalar2=None, op0=ALU.add),
          reads=[B_lb], writes=[B_lb])
    em.op("dve", lambda e: e.memset(lbT[:], 0.0), writes=[B_lb])
    em.op("dve", lambda e: e.reciprocal(out=lbT[:, 1, :], in_=tmpl[:]), reads=[B_lb], writes=[B_lb])
    em.op("dve", lambda e: e.tensor_scalar(out=omlT[:], in0=lbT[:], scalar1=-1.0, scalar2=1.0, op0=ALU.mult, op1=ALU.add),
          reads=[B_lb], writes=[B_lb])
    em.op("dve", lambda e: e.tensor_scalar(out=nomlT[:], in0=omlT[:], scalar1=-1.0, scalar2=None, op0=ALU.mult),
          reads=[B_lb], writes=[B_lb])
    for h in range(4):
        em.op("pool", lambda e, h=h: e.memset(Qz[h][:], 0.0), writes=[B_Qz[h]])
    for c in range(2):
        em.op("pool", lambda e, c=c: e.memset(Kc[c][:], 0.0), writes=B_Kc[c])
    em.op("pool", lambda e: e.memset(ATs[:], 0.0), writes=B_ATs)
    em.op("pool", lambda e: e.memset(S[:], 0.0), writes=B_S)
    em.op("pool", lambda e: e.memset(Sp[:], 0.0), writes=B_Sp)
    em.op("pool", lambda e: e.memset(gaT[:], 0.0), writes=[B_gaT])

    _chk("consts")
    cvt_i = [0]
    d_wh = [None]
    cast_engs = ("dve", "act", "pool")

    def convert(src_ap, dst_ap, n):
        i = cvt_i[0]
        cvt_i[0] += 1
        s32, b32 = cv32[i % 2], B_cv32[i % 2]
        k = i % 2
        shp = list(src_ap.shape[1:])
        if len(shp) == 2:
            v32 = s32[:, 0:n].rearrange("p (a b) -> p a b", a=shp[0])
            v16 = hid[:, k * 4096:k * 4096 + n].rearrange("p (a b) -> p a b", a=shp[0])
        else:
            v32 = s32[:, 0:n]
            v16 = hid[:, k * 4096:k * 4096 + n]
        b16 = B_cvh[k]
        em.dma(lambda sp, sem: sp.dma_start(out=v32, in_=src_ap).then_inc(sem, 16), 1, writes=[b32], owner=b32)
        ce = cast_engs[i % 3]
        if ce == "act":
            em.op("act", lambda e: e.activation(out=v16, in_=v32, func=AF.Copy), reads=[b32], writes=[b16])
        else:
            em.op(ce, lambda e: e.tensor_copy(out=v16, in_=v32), reads=[b32], writes=[b16])
        em.dma(lambda sp, sem: sp.dma_start(out=dst_ap, in_=v16).then_inc(sem, 16), 1, reads=[b16], writes=[d_wh[0]], owner=b16)

    B_cvh = [Buf("cvh0"), Buf("cvh1")]
    d_ws = {(l, k): Buf("dw_%d_%s" % (l, k)) for l in range(n_layers) for k in ("f1i", "f1o", "f2i", "f2o", "wi", "wo")}
    jobs = {}
    for l in range(n_layers):
        for nm, wsrc in (("f1", ffn1_w_in), ("f2", ffn2_w_in)):
            wv = wsrc[l].rearrange("(kc p) n -> p kc n", p=128)
            jobs[(l, nm + "i")] = [(wv[:, :, g * 512:(g + 1) * 512], s_win[nm][l][g].rearrange("p (kc n) -> p kc n", kc=8), 4096)
                                   for g in range(NG1)]
        wv = w_in[l].rearrange("(kc p) n -> p kc n", p=128)
        jobs[(l, "wi")] = [(wv[:, :, c0:c0 + ncol], s_wi[l][g].rearrange("p (kc n) -> p kc n", kc=8), 8 * ncol)
                           for g, (c0, ncol) in enumerate(WI_GROUPS)]
        for nm, wsrc in (("f1", ffn1_w_out), ("f2", ffn2_w_out)):
            wv = wsrc[l].rearrange("(kc p) n -> p kc n", p=128)
            dv = s_wout[nm][l].rearrange("(kc p) n -> p kc n", p=128)
            jobs[(l, nm + "o")] = [(wv[:, k0:k0 + min(4, 22 - k0), :], dv[:, k0:k0 + min(4, 22 - k0), :], min(4, 22 - k0) * 1024)
                                   for k0 in range(0, 22, 4)]
        wv = w_out[l].rearrange("(kc p) n -> p kc n", p=128)
        dv = s_wo[l].rearrange("(kc p) n -> p kc n", p=128)
        jobs[(l, "wo")] = [(wv[:, k0:k0 + 4, :], dv[:, k0:k0 + 4, :], 4096) for k0 in range(0, 8, 4)]
    eager_keys = [(0, "f1i"), (0, "f1o"), (0, "wi")]
    lazy_keys = [(0, "wo"), (0, "f2i"), (0, "f2o")] + [(l, k) for l in range(1, n_layers) for k in ("f1i", "f1o", "wi", "wo", "f2i", "f2o")]
    for key in eager_keys:
        d_wh[0] = d_ws[key]
        for (sa, da, n) in jobs[key]:
            convert(sa, da, n)

    def lazy_convert():
        xflat = xbs[1][:, :, :].rearrange("p b d -> p (b d)")
        gflat = gateT[:, :, :].rearrange("p u t -> p (u t)")
        i = 0
        for key in lazy_keys:
            dbuf = d_ws[key]
            for (sa, da, n) in jobs[key]:
                a3 = list(sa.shape[1:])[0]
                v32 = xflat[:, 0:n].rearrange("p (a b) -> p a b", a=a3)
                v16 = gflat[:, 0:n].rearrange("p (a b) -> p a b", a=a3)
                em.dma(lambda sp, sem, v32=v32, sa=sa: sp.dma_start(out=v32, in_=sa).then_inc(sem, 16), 1,
                       writes=[B_xbs[1]], owner=B_xbs[1])
                em.op("pool" if i % 2 else "dve", lambda e, v16=v16, v32=v32: e.tensor_copy(out=v16, in_=v32),
                      reads=[B_xbs[1]], writes=[B_gate])
                em.dma(lambda sp, sem, v16=v16, da=da: sp.dma_start(out=da, in_=v16).then_inc(sem, 16), 1,
                       reads=[B_gate], writes=[dbuf], owner=B_gate)
                i += 1
                yield
    em.barrier()
    _chk("convert")

    wr_i = [0]

    def wload(src_ap, n, shape3, dbuf):
        i = wr_i[0] % NSLOT
        wr_i[0] += 1
        view = wring[i][:, 0:n].rearrange("p (a b) -> p a b", a=shape3[0])
        em.dma(lambda sp, sem: sp.dma_start(out=view, in_=src_ap).then_inc(sem, 16), 1,
               reads=[dbuf], writes=[B_wr[i]], owner=B_wr[i])
        return B_wr[i], view

    mm_rot = [0]

    MMB = (4, 5, 6, 7)

    def nextbank():
        pb = MMB[mm_rot[0] % 4]
        mm_rot[0] += 1
        return pb

    def norm_to_hT(gain, l, xb, B_xb):
        xn = hid[:, 0:NB * 1024].rearrange("p (b d) -> p b d", b=NB)
        for b in range(NB):
            junk = hid[:, (NB + b) * 1024:(NB + b) * 1024 + 1024]
            em.op("act", lambda e, b=b, junk=junk: e.activation(out=junk, in_=xb[:, b, :], func=AF.Square,
                                                     accum_out=nstat[:, b:b + 1]),
                  reads=[B_xb], writes=[B_hid, B_nstat])
        em.op("act", lambda e: e.activation(out=nrstd[:, 0:NB], in_=nstat[:, 0:NB], func=AF.Ln, scale=1.0 / D, bias=epsc[:]),
              reads=[B_nstat, B_c2], writes=[B_nrstd])
        em.op("act", lambda e: e.activation(out=nrstd[:, 0:NB], in_=nrstd[:, 0:NB], func=AF.Exp, scale=-0.5),
              reads=[B_nrstd], writes=[B_nrstd])
        for b in range(NB):
            em.op("dve", lambda e, b=b: e.tensor_scalar(out=xn[:, b, :], in0=xb[:, b, :], scalar1=nrstd[:, b:b + 1],
                                                        scalar2=None, op0=ALU.mult),
                  reads=[B_xb, B_nrstd], writes=[B_hid])
        yield
        for kc in range(8):
            pb = 4 + (kc % 2)
            half = bank16(pb)[:, 0:TT]
            for b in range(NB):
                em.op("pe", lambda e, b=b, kc=kc, half=half: e.transpose(out=half[:, b * 128:(b + 1) * 128],
                                                                         in_=xn[:, b, kc * 128:(kc + 1) * 128],
                                                                         identity=ident[:]),
                      reads=[B_hid, B_c2], writes=[B_ps[pb]])
            if kc % 2 == 0:
                em.op("act", lambda e, kc=kc, half=half: e.activation(out=hT[:, kc, :], in_=half, func=AF.Copy,
                                                                      scale=gain[:, l, kc:kc + 1]),
                      reads=[B_ps[pb], B_c], writes=[B_hT])
            else:
                em.op("dve", lambda e, kc=kc, half=half: e.tensor_scalar(out=hT[:, kc, :], in0=half,
                                                                         scalar1=gain[:, l, kc:kc + 1], scalar2=None, op0=ALU.mult),
                      reads=[B_ps[pb], B_c], writes=[B_hT])
        yield

    def fmajor_mm(slot_view, c, pb, ncols=128, col0=None):
        c0 = c * 128 if col0 is None else col0
        for kc in range(8):
            em.op("pe", lambda e, kc=kc: e.matmul(bank(pb)[0:ncols, :], lhsT=slot_view[:, kc, c0:c0 + ncols],
                                                  rhs=hT[:, kc, :], start=(kc == 0), stop=(kc == 7)),
                  reads=[cur_w[0], B_hT], writes=[B_ps[pb]])

    cur_w = [None]

    def ffn(nm, l, xb, B_xb):
        hv = hid[:, :].rearrange("p (j t) -> p j t", j=22)
        for g in range(NG1):
            bw, wv = wload(s_win[nm][l][g].rearrange("p (kc n) -> p kc n", kc=8), 4096, (8, 512), d_ws[(l, nm + "i")])
            cur_w[0] = bw
            for c in range(4):
                j = g * 4 + c
                pb = nextbank()
                fmajor_mm(wv, c, pb)
                if j < 22:
                    em.op("act", lambda e, j=j, pb=pb: e.activation(out=hv[:, j, :], in_=bank(pb), func=AF.Silu),
                          reads=[B_ps[pb]], writes=[B_hid])
                else:
                    em.op("dve", lambda e, j=j, pb=pb: e.tensor_tensor(out=hv[:, j - 22, :], in0=bank(pb),
                                                                        in1=hv[:, j - 22, :], op=ALU.mult),
                          reads=[B_ps[pb], B_hid], writes=[B_hid])
            yield
        wov = s_wout[nm][l].rearrange("(kc p) n -> p kc n", p=128)
        for hf in range(2):
            for k0 in range(0, 22, 8):
                nk = min(8, 22 - k0)
                bw, wv = wload(wov[:, k0:k0 + nk, hf * 512:(hf + 1) * 512], nk * 512, (nk, 512), d_ws[(l, nm + "o")])
                for b in range(NB):
                    pb = MMB[b]
                    for kk in range(nk):
                        kc = k0 + kk
                        em.op("pe", lambda e, b=b, pb=pb, kk=kk, kc=kc, wv=wv: e.matmul(
                            bank(pb), lhsT=hv[:, kc, b * 128:(b + 1) * 128], rhs=wv[:, kk, :],
                            start=(kc == 0), stop=(kc == 21)),
                            reads=[bw, B_hid], writes=[B_ps[pb]])
                yield
            for b in range(NB):
                pb = MMB[b]
                em.op("dve", lambda e, b=b, hf=hf, pb=pb: e.scalar_tensor_tensor(
                    out=xb[:, b, hf * 512:(hf + 1) * 512], in0=bank(pb), scalar=0.5, op0=ALU.mult,
                    in1=xb[:, b, hf * 512:(hf + 1) * 512], op1=ALU.add),
                    reads=[B_ps[pb], B_xb], writes=[B_xb])
            yield

    def prep_common(grp, l, d, gs, kview_hg):
        ss = grp % NSET
        t_sg, t_g, t_b, t_k, t_A, t_tot, t_tmA = t_sg_l[ss], t_g_l[ss], t_b_l[ss], t_k_l[ss], t_A_l[ss], t_tot_l[ss], t_tmA_l[ss]
        B_sg, B_g, B_b, B_k, B_A, B_tot, B_tmA = B_sg_l[ss], B_g_l[ss], B_b_l[ss], B_k_l[ss], B_A_l[ss], B_tot_l[ss], B_tmA_l[ss]
        bflat = t_b[:, :, :].rearrange("p j t -> p (j t)")
        gflat = t_g[:, :, :].rearrange("p j t -> p (j t)")
        em.op("dve", lambda e: e.tensor_tensor_scan(out=bflat, data0=rmask[:], data1=gflat, initial=0.0,
                                                    op0=ALU.mult, op1=ALU.add),
              reads=[B_g, B_c], writes=[B_b])
        b4 = t_b[:, :, :].rearrange("p j (c t) -> p j c t", t=CH)
        em.op("pool", lambda e: e.tensor_copy(out=t_A[:], in_=b4[:, :, :, CH // 2 - 1]), reads=[B_b], writes=[B_A])
        em.op("pool", lambda e: e.tensor_copy(out=t_tot[:], in_=b4[:, :, :, CH - 1]), reads=[B_b], writes=[B_tot])
        em.op("pool", lambda e: e.tensor_tensor(out=t_tmA[:], in0=t_tot[:], in1=t_A[:], op=ALU.subtract),
              reads=[B_tot, B_A], writes=[B_tmA])
        if d == 1:
            em.op("pool", lambda e: e.tensor_tensor(out=t_b[:], in0=t_b[:], in1=t_g[:], op=ALU.subtract),
                  reads=[B_b, B_g], writes=[B_b])
        em.op("pool", lambda e: e.tensor_tensor(out=b4, in0=b4, in1=t_A[:].unsqueeze(3).to_broadcast([128, 2, NCH, CH]),
                                                op=ALU.subtract),
              reads=[B_b, B_A], writes=[B_b])
        sq, sk = (gs, -gs) if d == 0 else (-gs, gs)
        em.op("act", lambda e: e.activation(out=t_sg[:], in_=t_b[:], func=AF.Exp, scale=sq), reads=[B_b], writes=[B_sg])
        em.op("act", lambda e: e.activation(out=t_g[:], in_=t_b[:], func=AF.Exp, scale=sk), reads=[B_b], writes=[B_g])
        e_in_src, e_u_src = (t_A, t_tmA) if d == 0 else (t_tmA, t_A)
        em.op("act", lambda e: e.activation(out=esc[grp][:, 0], in_=e_in_src[:], func=AF.Exp, scale=gs),
              reads=[B_A, B_tmA], writes=[B_esc[grp]])
        em.op("act", lambda e: e.activation(out=esc[grp][:, 1], in_=t_tot[:], func=AF.Exp, scale=gs),
              reads=[B_tot], writes=[B_esc[grp]])
        em.op("act", lambda e: e.activation(out=esc[grp][:, 2], in_=e_u_src[:], func=AF.Exp, scale=gs),
              reads=[B_A, B_tmA], writes=[B_esc[grp]])
        if grp < 2:
            qv = qh[:, 2 * grp:2 * grp + 2, :]
            em.op("pool", lambda e: e.tensor_tensor(out=QdT[grp][:], in0=qv, in1=t_sg[:], op=ALU.mult),
                  reads=[B_qh, B_sg], writes=[B_Qd[grp]])
            em.op("pool", lambda e: e.tensor_tensor(out=KdT[grp][:], in0=t_k[:], in1=t_g[:], op=ALU.mult),
                  reads=[B_k, B_g], writes=[B_Kd[grp]])
        else:
            for h in range(4):
                j, p0 = h // 2, (h % 2) * 64
                em.op("pool", lambda e, h=h, j=j, p0=p0: e.tensor_tensor(out=Qz[h][p0:p0 + 64, :], in0=qg[p0:p0 + 64, j, :],
                                                                        in1=t_sg[p0:p0 + 64, j, :], op=ALU.mult),
                      reads=[B_qg, B_sg], writes=[B_Qz[h]])
            em.op("pool", lambda e: e.tensor_tensor(out=KdT[2][:], in0=kg[:], in1=t_g[:], op=ALU.mult),
                  reads=[B_kg, B_g], writes=[B_Kd[2]])


    def projections(l, d, want_gates, si, tix):
        wi_v = lambda g: s_wi[l][g].rearrange("p (kc n) -> p kc n", kc=8)
        qkv_parts = [(qh[:, :, :].rearrange("p a t -> p (a t)"), B_qh, 0, 4 * TT),
                     (qg[:, :, :].rearrange("p a t -> p (a t)"), B_qg, 4 * TT, 2 * TT),
                     (kg[:, :, :].rearrange("p a t -> p (a t)"), B_kg, 6 * TT, 2 * TT),
                     (Vt[:, :, :].rearrange("p a t -> p (a t)"), B_Vt, 8 * TT, NB * 1024)]
        if d == 1:
            for k, (view, bb, o0, n) in enumerate(qkv_parts):
                srcq = s_qkv[si][tix][:, o0:o0 + n]
                em.dma(lambda sp, sem, view=view, srcq=srcq: sp.dma_start(out=view, in_=srcq).then_inc(sem, 16), 1,
                       reads=[d_qkv[si][k]], writes=[bb], owner=bb)
        if d == 0:
            bw, wv = wload(wi_v(G_Q), 4096, (8, 512), d_ws[(l, "wi")]); cur_w[0] = bw
            for c in range(4):
                pb = nextbank()
                fmajor_mm(wv, c, pb)
                em.op("act", lambda e, c=c, pb=pb: e.activation(out=qh[:, c, :], in_=bank(pb), func=AF.Copy),
                      reads=[B_ps[pb]], writes=[B_qh])
            bw, wv = wload(wi_v(G_GQK), 4096, (8, 512), d_ws[(l, "wi")]); cur_w[0] = bw
            for c in range(4):
                pb = nextbank()
                fmajor_mm(wv, c, pb)
                if c < 2:
                    em.op("dve", lambda e, c=c, pb=pb: e.tensor_scalar(out=qg[:, c, :], in0=bank(pb), scalar1=0.125, scalar2=None, op0=ALU.mult),
                          reads=[B_ps[pb]], writes=[B_qg])
                else:
                    em.op("dve", lambda e, c=c, pb=pb: e.tensor_copy(out=kg[:, c - 2, :], in_=bank(pb)),
                          reads=[B_ps[pb]], writes=[B_kg])

        def vproj(gi, col):
            bw, wv = wload(wi_v(gi), 4096, (8, 512), d_ws[(l, "wi")])
            for b in range(NB):
                pb = nextbank()
                for kc in range(8):
                    em.op("pe", lambda e, b=b, kc=kc, pb=pb, wv=wv: e.matmul(bank(pb), lhsT=hT[:, kc, b * 128:(b + 1) * 128],
                                                                           rhs=wv[:, kc, :], start=(kc == 0), stop=(kc == 7)),
                          reads=[bw, B_hT], writes=[B_ps[pb]])
                em.op("dve", lambda e, b=b, pb=pb, col=col: e.tensor_copy(out=Vt[:, b, col:col + 512], in_=bank(pb)),
                      reads=[B_ps[pb]], writes=[B_Vt])

        def gproj(gi, u0, fn):
            bw, wv = wload(wi_v(gi), 4096, (8, 512), d_ws[(l, "wi")]); cur_w[0] = bw
            for c in range(4):
                pb = nextbank()
                fmajor_mm(wv, c, pb)
                em.op("act", lambda e, c=c, pb=pb, u0=u0, fn=fn: e.activation(out=gateT[:, u0 + c, :], in_=bank(pb), func=fn),
                      reads=[B_ps[pb]], writes=[B_gate])

        bwz, wvz = wload(wi_v(G_ZF if d == 0 else G_ZB), 4096, (8, 512), d_ws[(l, "wi")])
        for grp in range(2):
            cur_w[0] = bwz
            ss = grp % NSET
            t_sg, t_g, t_k = t_sg_l[ss], t_g_l[ss], t_k_l[ss]
            B_sg, B_g, B_k = B_sg_l[ss], B_g_l[ss], B_k_l[ss]
            for jj in range(2):
                c = grp * 2 + jj
                pb = nextbank()
                fmajor_mm(wvz, c, pb)
                em.op("act", lambda e, jj=jj, pb=pb, t_sg=t_sg: e.activation(out=t_sg[:, jj, :], in_=bank(pb), func=AF.Sigmoid),
                      reads=[B_ps[pb]], writes=[B_sg])
            for jj in range(2):
                col = d * 4 + grp * 2 + jj
                em.op("act", lambda e, jj=jj, col=col, t_g=t_g, t_sg=t_sg: e.activation(out=t_g[:, jj, :], in_=t_sg[:, jj, :], func=AF.Ln,
                                                                    scale=omlT[:, l, col:col + 1], bias=lbT[:, l, col:col + 1]),
                      reads=[B_sg, B_lb], writes=[B_g])
                em.op("dve", lambda e, jj=jj, col=col, t_k=t_k, t_sg=t_sg: e.tensor_scalar(out=t_k[:, jj, :], in0=t_sg[:, jj, :],
                                                                        scalar1=nomlT[:, l, col:col + 1],
                                                                        scalar2=omlT[:, l, col:col + 1],
                                                                        op0=ALU.mult, op1=ALU.add),
                      reads=[B_sg, B_lb], writes=[B_k])
            prep_common(grp, l, d, 1.0, None)
            if d == 0:
                if grp == 0:
                    vproj(G_VH, 0)
                else:
                    vproj(G_VG, 512)
            elif want_gates:
                if grp == 0:
                    gproj(G_HG, 0, AF.Sigmoid)
                else:
                    gproj(G_GR, 4, AF.Silu)
        bw, wv = wload(wi_v(G_GA), 8 * 144, (8, 144), d_ws[(l, "wi")]); cur_w[0] = bw
        ss = 2 % NSET
        t_sg, t_g = t_sg_l[ss], t_g_l[ss]
        B_sg, B_g = B_sg_l[ss], B_g_l[ss]
        pb = nextbank()
        fmajor_mm(wv, 0, pb, ncols=128, col0=16 * d)
        em.op("act", lambda e, pb=pb: e.activation(out=gaT[0:16, :], in_=bank(pb)[0:16, :], func=AF.Copy),
              reads=[B_ps[pb]], writes=[B_gaT])
        for j in range(2):
            pb = nextbank()
            em.op("pe", lambda e, j=j, pb=pb: e.matmul(bank(pb), lhsT=wgb[:, l, d, j * 128:(j + 1) * 128], rhs=gaT[:, :],
                                                        start=True, stop=True),
                  reads=[B_gaT, B_c3], writes=[B_ps[pb]])
            em.op("act", lambda e, j=j, pb=pb, t_sg=t_sg: e.activation(out=t_sg[:, j, :], in_=bank(pb), func=AF.Sigmoid,
                                                            bias=bgT[:, l, d, j:j + 1]),
                  reads=[B_ps[pb], B_c], writes=[B_sg])
        em.op("act", lambda e, t_g=t_g, t_sg=t_sg: e.activation(out=t_g[:], in_=t_sg[:], func=AF.Ln), reads=[B_sg], writes=[B_g])
        prep_common(2, l, d, 1.0 / 16.0, None)
        if d == 0:
            for k, (view, bb, o0, n) in enumerate(qkv_parts):
                dstq = s_qkv[si][tix][:, o0:o0 + n]
                em.dma(lambda sp, sem, view=view, dstq=dstq: sp.dma_start(out=dstq, in_=view).then_inc(sem, 16), 1,
                       reads=[bb], writes=[d_qkv[si][k]], owner=bb)
        if want_gates:
            em.op("pool", lambda e: e.tensor_tensor(out=gateT[:], in0=gateT[:],
                                                    in1=g_hn[:, l, :].unsqueeze(2).to_broadcast([128, 8, TT]), op=ALU.mult),
                  reads=[B_gate, B_c], writes=[B_gate])

    def unit(u):
        if u < 4:
            grp, j = u // 2, u % 2
            return dict(qrhs=lambda t0, n: QdT[grp][:, j, t0:t0 + n], qbuf=B_Qd[grp],
                        klhs=lambda t0: KdT[grp][:, j, t0:t0 + 128], kbuf=B_Kd[grp],
                        kc_cols=(u * 128, u * 128 + 128), si=u, p0=0, pn=128, vcol=u * 128, grp=grp, j=j)
        h = u - 4
        j, p0 = h // 2, (h % 2) * 64
        return dict(qrhs=lambda t0, n: Qz[h][:, t0:t0 + n], qbuf=B_Qz[h],
                    klhs=lambda t0: KdT[2][:, j, t0:t0 + 128], kbuf=B_Kd[2],
                    kc_cols=(512 + j * 128, 512 + j * 128 + 128), si=4 + j, p0=p0, pn=64, vcol=512 + h * 128, grp=2, j=j)

    units = [unit(u) for u in range(8)]
    AT_B, O_BK, U_BS, TR_B = 0, 1, (2, 3), 3

    def scan_half(l, d, b, hb, first_chunk_of_seq, evac_fn):
        tb = b * 128
        us = list(range(hb * 4, hb * 4 + 4))
        corder = (0, 1) if d == 0 else (1, 0)

        def uinfo(u, ci):
            un = units[u]
            return (un, un["p0"], un["pn"], un["si"], esc[un["grp"]], B_S[un["si"]], B_Sp[un["si"]],
                    bank(U_BS[ci])[:, (u % 4) * 128:(u % 4) * 128 + 128])

        def emit_sp(ci):
            c = corder[ci]
            cl = (tb // CH) + c
            for u in us:
                un, p0, pn, sidx, e3, bS, bSp, ureg = uinfo(u, ci)
                em.op("act", lambda e, p0=p0, pn=pn, sidx=sidx, e3=e3, un=un, cl=cl: e.activation(
                    out=Sp[p0:p0 + pn, sidx, :], in_=S[p0:p0 + pn, sidx, :], func=AF.Copy,
                    scale=e3[p0:p0 + pn, 0, un["j"], cl:cl + 1]),
                    reads=[bS, B_esc[un["grp"]]], writes=[bSp])

        def emit_state_mm(ci):
            c = corder[ci]
            for u in us:
                un, p0, pn, sidx, e3, bS, bSp, ureg = uinfo(u, ci)
                em.op("pe", lambda e, u=u, un=un, sidx=sidx, c=c: e.matmul(
                    bank(O_BK)[:, (u % 4) * 128 + c * CH:(u % 4) * 128 + c * CH + CH],
                    lhsT=Sp[:, sidx, :], rhs=un["qrhs"](tb + c * CH, CH), start=False, stop=False, skip_group_check=True),
                    reads=[bSp, un["qbuf"]], writes=[B_ps[O_BK]])

        def emit_u_and_update(ci, first):
            c = corder[ci]
            cl = (tb // CH) + c
            pbu = U_BS[ci]
            for u in us:
                un, p0, pn, sidx, e3, bS, bSp, ureg = uinfo(u, ci)
                c0, c1 = un["kc_cols"]
                em.op("pe", lambda e, un=un, ureg=ureg, c=c, c0=c0, c1=c1: e.matmul(
                    ureg, lhsT=Kc[c][:, c0:c1], rhs=Vt[:, b, un["vcol"]:un["vcol"] + 128], start=True, stop=True),
                    reads=[B_Kc[c][hb], B_Vt], writes=[B_ps[pbu]])
            for u in us:
                un, p0, pn, sidx, e3, bS, bSp, ureg = uinfo(u, ci)
                sl = S[p0:p0 + pn, sidx, :]
                if first:
                    em.op("dve", lambda e, sl=sl, ureg=ureg, p0=p0, pn=pn, e3=e3, un=un, cl=cl: e.tensor_scalar(
                        out=sl, in0=ureg[p0:p0 + pn, :], scalar1=e3[p0:p0 + pn, 2, un["j"], cl:cl + 1], scalar2=None, op0=ALU.mult),
                        reads=[B_ps[pbu], B_esc[un["grp"]]], writes=[bS])
                else:
                    em.op("pool", lambda e, sl=sl, p0=p0, pn=pn, e3=e3, un=un, cl=cl: e.tensor_scalar(
                        out=sl, in0=sl, scalar1=e3[p0:p0 + pn, 1, un["j"], cl:cl + 1], scalar2=1.0, op0=ALU.mult, op1=ALU.mult),
                        reads=[bS, B_esc[un["grp"]]], writes=[bS])
                    em.op("dve", lambda e, sl=sl, ureg=ureg, p0=p0, pn=pn, e3=e3, un=un, cl=cl: e.scalar_tensor_tensor(
                        out=sl, in0=ureg[p0:p0 + pn, :], scalar=e3[p0:p0 + pn, 2, un["j"], cl:cl + 1], op0=ALU.mult,
                        in1=sl, op1=ALU.add),
                        reads=[B_ps[pbu], bS, B_esc[un["grp"]]], writes=[bS])

        for u in us:
            un = units[u]
            em.op("pe", lambda e, u=u, un=un: e.matmul(bank(AT_B)[:, (u % 4) * 128:(u % 4) * 128 + 128],
                                                       lhsT=un["klhs"](tb), rhs=un["qrhs"](tb, 128), start=True, stop=True),
                  reads=[un["kbuf"], un["qbuf"]], writes=[B_ps[AT_B]])
        tr = bank16(TR_B)
        tlist = [0, 1, 2, 3] if hb == 0 else [4, 5]
        for i in tlist:
            src = KdT[i // 2][:, i % 2, tb:tb + 128]
            em.op("pe", lambda e, i=i, src=src: e.transpose(out=tr[:, i * 128:(i + 1) * 128], in_=src, identity=ident[:]),
                  reads=[B_Kd[i // 2], B_c2], writes=[B_ps[TR_B]])
        k0c, k1c = (0, 512) if hb == 0 else (512, 768)
        for c in range(2):
            r0 = c * CH
            em.op("act", lambda e, c=c, r0=r0: e.activation(out=Kc[c][r0:r0 + CH, k0c:k1c], in_=tr[r0:r0 + CH, k0c:k1c], func=AF.Copy),
                  reads=[B_ps[TR_B]], writes=[B_Kc[c][hb]])
        for c in range(2):
            r0 = c * CH
            src = bank(AT_B)[r0:r0 + CH, :].rearrange("p (h t) -> p h t", h=4)[:, :, r0:r0 + CH]
            dst = ATs[r0:r0 + CH, hb * 4:hb * 4 + 4, r0:r0 + CH]
            mk = mask[r0:r0 + CH, d, :].unsqueeze(1).to_broadcast([CH, 4, CH])
            em.op("dve", lambda e, src=src, dst=dst, mk=mk: e.tensor_tensor(out=dst, in0=src, in1=mk, op=ALU.mult),
                  reads=[B_ps[AT_B], B_c], writes=[B_ATs[hb]])
        if not first_chunk_of_seq:
            emit_sp(0)
        yield
        for u in us:
            un = units[u]
            em.op("pe", lambda e, u=u, un=un: e.matmul(bank(O_BK)[:, (u % 4) * 128:(u % 4) * 128 + 128],
                                                       lhsT=Vt[:, b, un["vcol"]:un["vcol"] + 128], rhs=ATs[:, u, :],
                                                       start=(u % 4 == 0), stop=False, skip_group_check=True),
                  reads=[B_Vt, B_ATs[hb]], writes=[B_ps[O_BK]])
        if not first_chunk_of_seq:
            emit_state_mm(0)
        emit_u_and_update(0, first_chunk_of_seq)
        emit_sp(1)
        yield
        emit_state_mm(1)
        emit_u_and_update(1, False)
        evac_fn()
        yield

    def flat(ap):
        return ap.rearrange("p u t -> p (u t)")

    def scan_tile(si, l, d, t0, first_tile):
        border = list(range(NB)) if d == 0 else list(range(NB - 1, -1, -1))
        for bi, b in enumerate(border):
            gb = (t0 // 128) + b
            if d == 1:
                srcv2 = s_of[si][gb]
                em.dma(lambda sp, sem, srcv2=srcv2: sp.dma_start(out=flat(ofl[:]), in_=srcv2).then_inc(sem, 16),
                       1, reads=[d_of[si]], writes=[B_ofl], owner=B_ofl)
            for hb in range(2):
                oh = flat(osb[:, hb * 4:hb * 4 + 4, :])

                def evac_fn(oh=oh, hb=hb):
                    if d == 0:
                        if hb == 0:
                            em.op("act", lambda e: e.activation(out=oh, in_=bank(O_BK), func=AF.Copy),
                                  reads=[B_ps[O_BK]], writes=[B_osb])
                        else:
                            em.op("dve", lambda e: e.tensor_copy(out=oh, in_=bank(O_BK)), reads=[B_ps[O_BK]], writes=[B_osb])
                    else:
                        em.op("dve", lambda e: e.tensor_tensor(out=oh, in0=bank(O_BK), in1=flat(ofl[:, hb * 4:hb * 4 + 4, :]), op=ALU.add),
                              reads=[B_ps[O_BK], B_ofl], writes=[B_osb])
                yield from scan_half(l, d, b, hb, (first_tile and bi == 0), evac_fn)
            if d == 0:
                dstv = s_of[si][gb]
                em.dma(lambda sp, sem, dstv=dstv: sp.dma_start(out=dstv, in_=flat(osb[:])).then_inc(sem, 16),
                       1, reads=[B_osb], writes=[d_of[si]], owner=B_osb)
            else:
                em.op("act", lambda e: e.activation(out=sqb[:], in_=osb[:], func=AF.Square), reads=[B_osb], writes=[B_sqb])
                for hb in range(2):
                    em.op("pe", lambda e, hb=hb: e.matmul(bank(AT_B), lhsT=ones_b[:], rhs=flat(sqb[:, hb * 4:hb * 4 + 4, :]),
                                                          start=True, stop=True),
                          reads=[B_sqb, B_c2], writes=[B_ps[AT_B]])
                    em.op("act", lambda e, hb=hb: e.activation(out=flat(rsb[:, hb * 4:hb * 4 + 4, :]), in_=bank(AT_B), func=AF.Ln,
                                                               scale=1.0 / 128.0, bias=epsc[:]),
                          reads=[B_ps[AT_B], B_c2], writes=[B_rsb])
                em.op("act", lambda e: e.activation(out=rsb[:], in_=rsb[:], func=AF.Exp, scale=-0.5), reads=[B_rsb], writes=[B_rsb])
                em.op("pool", lambda e: e.tensor_tensor(out=osb[:], in0=osb[:], in1=rsb[:], op=ALU.mult),
                      reads=[B_osb, B_rsb], writes=[B_osb])
                em.op("pool", lambda e, b=b: e.tensor_tensor(out=onT[:, :, b * 128:(b + 1) * 128], in0=osb[:],
                                                             in1=gateT[:, :, b * 128:(b + 1) * 128], op=ALU.mult),
                      reads=[B_osb, B_gate], writes=[B_onT])
            yield

    def interleave(*gs):
        gens = [g for g in gs if g is not None]
        while gens:
            for g in list(gens):
                try:
                    next(g)
                except StopIteration:
                    gens.remove(g)

    def run(g):
        for _ in g:
            pass

    def xview(ap2d, t0):
        return ap2d[t0:t0 + TT, :].rearrange("(b p) d -> p b d", p=128)

    def preload_fwd(si, l, t0, x_in):
        xb, Bx = xbs[0], B_xbs[0]
        src = x_in if l == 0 else s_x[si]
        srcv = xview(src, t0)
        em.dma(lambda sp, sem: sp.dma_start(out=xb[:], in_=srcv).then_inc(sem, 16), 1,
               reads=([] if l == 0 else [d_x[si]]), writes=[Bx], owner=Bx)

    def ffn_fwd_stage(si, l, t0):
        xb, Bx = xbs[0], B_xbs[0]
        yield from norm_to_hT(g_n1, l, xb, Bx)
        yield from ffn("f1", l, xb, Bx)
        dstv = xview(s_x[si], t0)
        em.dma(lambda sp, sem: sp.dma_start(out=dstv, in_=xb[:]).then_inc(sem, 16), 1,
               reads=[Bx], writes=[d_x[si]], owner=Bx)
        yield
        yield from norm_to_hT(g_nm, l, xb, Bx)

    def preload_bwd(si, t0, k):
        xb, Bx = xbs[k], B_xbs[k]
        srcv = xview(s_x[si], t0)
        em.dma(lambda sp, sem: sp.dma_start(out=xb[:], in_=srcv).then_inc(sem, 16), 1,
               reads=[d_x[si]], writes=[Bx], owner=Bx)

    def chain(*gens):
        for g in gens:
            if g is not None:
                yield from g

    def gen_call(fn, *args):
        fn(*args)
        yield

    def ffn_bwd_stage(si, l, t0, k, y_out):
        xb, Bx = xbs[k], B_xbs[k]
        yield from norm_to_hT(g_n2, l, xb, Bx)
        yield from ffn("f2", l, xb, Bx)
        if l < n_layers - 1:
            dstv = xview(s_x[si], t0)
            em.dma(lambda sp, sem: sp.dma_start(out=dstv, in_=xb[:]).then_inc(sem, 16), 1,
                   reads=[Bx], writes=[d_x[si]], owner=Bx)
        else:
            for b in range(NB):
                junk = hid[:, b * 1024:b * 1024 + 1024]
                em.op("act", lambda e, b=b, junk=junk: e.activation(out=junk, in_=xb[:, b, :], func=AF.Square,
                                                                    accum_out=nstat[:, b:b + 1]),
                      reads=[Bx], writes=[B_hid, B_nstat])
            em.op("act", lambda e: e.activation(out=nrstd[:, 0:NB], in_=nstat[:, 0:NB], func=AF.Ln, scale=1.0 / D, bias=epsc[:]),
                  reads=[B_nstat, B_c2], writes=[B_nrstd])
            em.op("act", lambda e: e.activation(out=nrstd[:, 0:NB], in_=nrstd[:, 0:NB], func=AF.Exp, scale=-0.5),
                  reads=[B_nrstd], writes=[B_nrstd])
            for b in range(NB):
                em.op("dve", lambda e, b=b: e.scalar_tensor_tensor(out=xb[:, b, :], in0=xb[:, b, :], scalar=nrstd[:, b:b + 1],
                                                                    op0=ALU.mult, in1=g_fin[:], op1=ALU.mult),
                      reads=[Bx, B_nrstd, B_c], writes=[Bx])
            dstv = xview(y_out, t0)
            em.dma(lambda sp, sem: sp.dma_start(out=dstv, in_=xb[:]).then_inc(sem, 16), 1,
                   reads=[Bx], writes=[], owner=Bx)
        yield

    for si, (T, x_in, y_out) in enumerate(seqs):
        ntile = T // TT
        for l in range(n_layers):
            lazy = lazy_convert() if (si == 0 and l == 0) else None
            preload_fwd(si, l, 0, x_in)
            interleave(ffn_fwd_stage(si, l, 0), lazy)
            for ti in range(ntile):
                has_next = ti + 1 < ntile
                projections(l, 0, False, si, ti)
                _chk("proj")
                if has_next:
                    preload_fwd(si, l, (ti + 1) * TT, x_in)
                nxt = ffn_fwd_stage(si, l, (ti + 1) * TT) if has_next else None
                interleave(scan_tile(si, l, 0, ti * TT, first_tile=(ti == 0)), nxt, lazy)
                _chk("fwdpass")
            if lazy is not None:
                run(lazy)
            prev = None
            border_t = list(range(ntile - 1, -1, -1))
            preload_bwd(si, border_t[0] * TT, 0)
            run(norm_to_hT(g_nm, l, xbs[0], B_xbs[0]))
            for oi, ti in enumerate(border_t):
                k = oi % 2
                xb, Bx = xbs[k], B_xbs[k]
                has_next = oi + 1 < ntile
                projections(l, 1, True, si, ti)
                pg = ffn_bwd_stage(si, l, prev[0] * TT, prev[1], y_out) if prev is not None else None
                pl = gen_call(preload_bwd, si, border_t[oi + 1] * TT, 1 - k) if has_next else None
                nn = norm_to_hT(g_nm, l, xbs[1 - k], B_xbs[1 - k]) if has_next else None
                interleave(scan_tile(si, l, 1, ti * TT, first_tile=(oi == 0)), chain(pg, pl, nn))
                wov = s_wo[l].rearrange("(kc p) n -> p kc n", p=128)
                slots = [wload(wov[:, k0:k0 + 4, :], 4096, (4, 1024), d_ws[(l, "wo")]) for k0 in (0, 4)]
                for b in range(NB):
                    for hf in range(2):
                        pb = nextbank()
                        for kc in range(8):
                            bw, wv = slots[kc // 4]
                            em.op("pe", lambda e, b=b, hf=hf, pb=pb, kc=kc, wv=wv: e.matmul(
                                bank(pb), lhsT=onT[:, kc, b * 128:(b + 1) * 128], rhs=wv[:, kc % 4, hf * 512:(hf + 1) * 512],
                                start=(kc == 0), stop=(kc == 7)),
                                reads=[bw, B_onT], writes=[B_ps[pb]])
                        em.op("dve", lambda e, b=b, hf=hf, pb=pb, xb=xb: e.tensor_tensor(
                            out=xb[:, b, hf * 512:(hf + 1) * 512], in0=bank(pb), in1=xb[:, b, hf * 512:(hf + 1) * 512], op=ALU.add),
                            reads=[B_ps[pb], Bx], writes=[Bx])
                prev = (ti, k)
            run(ffn_bwd_stage(si, l, prev[0] * TT, prev[1], y_out))


def make_consts():
    ident = np.eye(128, dtype=np.float32)
    p = np.arange(128)[:, None] % CH
    t = np.arange(CH)[None, :]
    mask = np.stack([(p <= t), (p >= t)], axis=1).astype(np.float32)
    rm = np.ones((2 * TT,), np.float32)
    rm[::CH] = 0.0
    import ml_dtypes
    rmask = np.broadcast_to(rm, (128, 2 * TT)).astype(ml_dtypes.bfloat16)
    return {"c_ident": ident, "c_mask": mask, "c_rmask": rmask}


_W_NAMES = ["lower_bounds", "ffn1_norm", "ffn1_w_in", "ffn1_w_out", "mix_norm", "w_in", "gla_w_gate", "gla_b_gate",
            "hg_head_norm", "gla_head_norm", "w_out", "ffn2_norm", "ffn2_w_in", "ffn2_w_out", "final_norm"]


def kernel(**inputs):
    n = 8
    xp = np.ascontiguousarray(inputs["x_prompt"], dtype=np.float32)
    xs = np.ascontiguousarray(inputs["x_sample"], dtype=np.float32)
    BP, TP, _ = xp.shape
    BS, TS, _ = xs.shape
    npc, nsc = BP // n, BS // n
    nc = build_program((npc, TP), (nsc, TS))
    consts = make_consts()
    shared = {k: np.ascontiguousarray(inputs[k], dtype=np.float32) for k in _W_NAMES}
    in_maps = []
    for c in range(n):
        m = dict(shared)
        m.update(consts)
        m["xp"] = xp[c * npc:(c + 1) * npc]
        m["xs"] = xs[c * nsc:(c + 1) * nsc]
        in_maps.append(m)
    res = run_bass_kernel_spmd(nc, in_maps, core_ids=list(range(n)))
    yp = np.concatenate([r["yp"] for r in res.results], axis=0).astype(np.float32)
    ys = np.concatenate([r["ys"] for r in res.results], axis=0).astype(np.float32)
    return (yp, ys)
```
